# Optimizing a Trainium2 kernel written in Bass

```python
import math
import jax, jax.numpy as jnp
from jax import lax
import numpy as np

D_MODEL = 2048
BATCH = 2
SEQ = 16384
DEPTH = 2
DEC_BATCH = 2
DEC_SEQ = 4096
PAST_LEN = 128

N_MIXERS = 2
N_FOURIER = (DEPTH + 1) // 2
N_ATTN = DEPTH // 2
FFT_GROUPS = 8
FFT_GROUP_DIM = D_MODEL // FFT_GROUPS
HEAD_DIM = 64
N_HEADS = D_MODEL // HEAD_DIM
N_KV = 4
GROUP = N_HEADS // N_KV
Q_DIM = N_HEADS * HEAD_DIM
KV_DIM = N_KV * HEAD_DIM
QKV_DIM = Q_DIM + 2 * KV_DIM
WINDOW = 128
BLOCK = 128
ROPE_THETA = 500000.0
ROT_DIM = HEAD_DIM // 4
ROT_HALF = ROT_DIM // 2
D_FF = 4 * D_MODEL
EPS = 1e-6
ATTN_SCALE = 1.0 / math.sqrt(HEAD_DIM)
NEG = -1e30

kernel_name = "fnet_swa_sink_hybrid_encoder"


def _rmsnorm(x, g):
    xf = x.astype(jnp.float32)
    y = xf * lax.rsqrt(jnp.mean(xf * xf, axis=-1, keepdims=True) + EPS)
    return (y * g.astype(jnp.float32)).astype(x.dtype)


def _rope_tables(s):
    inv_freq = ROPE_THETA ** (-jnp.arange(0, ROT_DIM, 2, dtype=jnp.float32) / ROT_DIM)
    ang = jnp.arange(s, dtype=jnp.float32)[:, None] * inv_freq[None, :]
    return jnp.cos(ang), jnp.sin(ang)


def _partial_rope(t, cos, sin):
    tf = t.astype(jnp.float32)
    c = cos[None, :, None, :]
    sn = sin[None, :, None, :]
    x1 = tf[..., :ROT_HALF]
    x2 = tf[..., ROT_HALF:ROT_DIM]
    out = jnp.concatenate([x1 * c - x2 * sn, x2 * c + x1 * sn, tf[..., ROT_DIM:]], axis=-1)
    return out.astype(t.dtype)


def _fourier_mixer(h, w_out):
    b, s, d = h.shape
    hg = h.astype(jnp.float32).reshape(b, s, FFT_GROUPS, FFT_GROUP_DIM)
    f = jnp.fft.fft2(hg, axes=(1, 3), norm="ortho").real
    return f.reshape(b, s, d).astype(h.dtype) @ w_out


def _window_attention(h, w_qkv, q_gain, k_gain, sink, w_o):
    b, s, _ = h.shape
    qkv = h @ w_qkv
    q = qkv[..., :Q_DIM].reshape(b, s, N_HEADS, HEAD_DIM)
    k = qkv[..., Q_DIM:Q_DIM + KV_DIM].reshape(b, s, N_KV, HEAD_DIM)
    v = qkv[..., Q_DIM + KV_DIM:].reshape(b, s, N_KV, HEAD_DIM)
    q = _rmsnorm(q, q_gain)
    k = _rmsnorm(k, k_gain)
    cos, sin = _rope_tables(s)
    q = _partial_rope(q, cos, sin)
    k = _partial_rope(k, cos, sin)

    nb = s // BLOCK
    qb = jnp.moveaxis(q.reshape(b, nb, BLOCK, N_KV, GROUP, HEAD_DIM), 1, 0)
    pad = ((0, 0), (BLOCK, BLOCK), (0, 0), (0, 0))
    kp = jnp.pad(k, pad).reshape(b, nb + 2, BLOCK, N_KV, HEAD_DIM)
    vp = jnp.pad(v, pad).reshape(b, nb + 2, BLOCK, N_KV, HEAD_DIM)
    kw = jnp.moveaxis(jnp.concatenate([kp[:, :-2], kp[:, 1:-1], kp[:, 2:]], axis=2), 1, 0)
    vw = jnp.moveaxis(jnp.concatenate([vp[:, :-2], vp[:, 1:-1], vp[:, 2:]], axis=2), 1, 0)

    qi = jnp.arange(BLOCK)[:, None]
    kj = jnp.arange(3 * BLOCK)[None, :]
    band = jnp.abs(qi + BLOCK - kj) <= WINDOW
    sink_f = sink.astype(jnp.float32).reshape(N_KV, GROUP)[None, :, :, None, None]

    def block_fn(args):
        blk, q_b, k_b, v_b = args
        key_pos = blk * BLOCK - BLOCK + kj
        mask = band & (key_pos >= 0) & (key_pos < s)
        sc = jnp.einsum('bqhgd,bkhd->bhgqk', q_b.astype(jnp.float32), k_b.astype(jnp.float32)) * ATTN_SCALE
        sc = jnp.where(mask, sc, NEG)
        m = jnp.maximum(jnp.max(sc, axis=-1, keepdims=True), sink_f)
        p = jnp.exp(sc - m)
        denom = jnp.sum(p, axis=-1, keepdims=True) + jnp.exp(sink_f - m)
        o = jnp.einsum('bhgqk,bkhd->bqhgd', p / denom, v_b.astype(jnp.float32))
        return o.astype(h.dtype)

    o = lax.map(block_fn, (jnp.arange(nb), qb, kw, vw))
    o = jnp.moveaxis(o, 0, 1).reshape(b, s, Q_DIM)
    return o @ w_o


def _sqrelu_mlp(h, w_up, w_down):
    a = jax.nn.relu(h @ w_up)
    return (a * a) @ w_down


def _trunk(x, fourier_norm, fourier_w_out, attn_norm, attn_w_qkv, attn_q_norm, attn_k_norm,
           attn_sink, attn_w_o, mlp_norm, mlp_w_up, mlp_w_down):
    for i in range(DEPTH):
        j = i // N_MIXERS
        if i % N_MIXERS == 0:
            x = x + _fourier_mixer(_rmsnorm(x, fourier_norm[j]), fourier_w_out[j])
        else:
            x = x + _window_attention(_rmsnorm(x, attn_norm[j]), attn_w_qkv[j], attn_q_norm[j],
                                      attn_k_norm[j], attn_sink[j], attn_w_o[j])
        x = x + _sqrelu_mlp(_rmsnorm(x, mlp_norm[i]), mlp_w_up[i], mlp_w_down[i])
    return x


def setup_inputs(seed: int = 0) -> dict:
    key = jax.random.key(seed)
    ks = jax.random.split(key, 14)
    f32 = jnp.float32
    nrm = jax.random.normal
    return {
        "x_prompt": nrm(ks[0], (BATCH, SEQ, D_MODEL), f32),
        "x_sample": nrm(ks[1], (DEC_BATCH, DEC_SEQ, D_MODEL), f32),
        "fourier_norm": 1.0 + 0.02 * nrm(ks[2], (N_FOURIER, D_MODEL), f32),
        "fourier_w_out": nrm(ks[3], (N_FOURIER, D_MODEL, D_MODEL), f32) * D_MODEL ** -0.5,
        "attn_norm": 1.0 + 0.02 * nrm(ks[4], (N_ATTN, D_MODEL), f32),
        "attn_w_qkv": nrm(ks[5], (N_ATTN, D_MODEL, QKV_DIM), f32) * D_MODEL ** -0.5,
        "attn_q_norm": 1.0 + 0.02 * nrm(ks[6], (N_ATTN, HEAD_DIM), f32),
        "attn_k_norm": 1.0 + 0.02 * nrm(ks[7], (N_ATTN, HEAD_DIM), f32),
        "attn_sink": 0.5 * nrm(ks[8], (N_ATTN, N_HEADS), f32),
        "attn_w_o": nrm(ks[9], (N_ATTN, Q_DIM, D_MODEL), f32) * Q_DIM ** -0.5,
        "mlp_norm": 1.0 + 0.02 * nrm(ks[10], (DEPTH, D_MODEL), f32),
        "mlp_w_up": nrm(ks[11], (DEPTH, D_MODEL, D_FF), f32) * D_MODEL ** -0.5,
        "mlp_w_down": nrm(ks[12], (DEPTH, D_FF, D_MODEL), f32) * D_FF ** -0.5,
    }


def reference(x_prompt, x_sample, fourier_norm, fourier_w_out, attn_norm, attn_w_qkv, attn_q_norm,
              attn_k_norm, attn_sink, attn_w_o, mlp_norm, mlp_w_up, mlp_w_down):
    y_prompt = _trunk(x_prompt, fourier_norm, fourier_w_out, attn_norm, attn_w_qkv, attn_q_norm,
                      attn_k_norm, attn_sink, attn_w_o, mlp_norm, mlp_w_up, mlp_w_down)
    y_sample = _trunk(x_sample, fourier_norm, fourier_w_out, attn_norm, attn_w_qkv, attn_q_norm,
                      attn_k_norm, attn_sink, attn_w_o, mlp_norm, mlp_w_up, mlp_w_down)
    return (y_prompt, y_sample)
```

```python
import math
import contextlib
import numpy as np
import ml_dtypes
import concourse.bass as bass
import concourse.mybir as mybir
from concourse.bass_utils import run_bass_kernel_spmd

F32 = mybir.dt.float32
BF16 = mybir.dt.bfloat16
ACT = mybir.ActivationFunctionType
ALU = mybir.AluOpType
AX = mybir.AxisListType

D = 2048
DC = 16
DFF = 8192
SEQ_P, SEQ_S = 16384, 4096
OWN_P, OWN_S = 4096, 1024
K1_P, K1_S = 34, 40
N2_P, N2_S = 128, 32
EXT_P, EXT_S = K1_P * N2_P, K1_S * N2_S
EXT = EXT_P + EXT_S
OWN = OWN_P + OWN_S
EPS = 1e-6
NEGM = -30000.0
NDS = 8


class Res:
    __slots__ = ("name", "w", "rs", "const")

    def __init__(self, name, const=False):
        self.name, self.w, self.rs, self.const = name, None, [], const


class Sched:
    ENG = ["pe", "act", "dve", "pool", "sp"]

    def __init__(self, nc, stack):
        self.nc = nc
        self.ops = {e: [] for e in self.ENG}
        self.cnt = {e: 0 for e in self.ENG}
        self.esem = {e: stack.enter_context(nc.semaphore("es_" + e)) for e in ["pe", "act", "dve", "pool"]}
        self.dsem = {q: [stack.enter_context(nc.semaphore("ds_%s%d" % (q, i))) for i in range(NDS)] for q in ["sp", "pool"]}
        self.duse = {q: [0] * NDS for q in ["sp", "pool"]}
        self.dnext = {"sp": 0, "pool": 0}
        self.waited = {e: {} for e in self.ENG}

    def _collect(self, reads, writes):
        deps = set()
        for r in reads:
            if r.w is not None:
                deps.add(r.w)
        for r in writes:
            if r.w is not None:
                deps.add(r.w)
            deps.update(r.rs)
        return deps

    def _update(self, reads, writes, tok):
        for r in reads:
            if not r.const:
                r.rs.append(tok)
        for r in writes:
            r.w = tok
            r.rs = []

    def _waits(self, eng, deps):
        out = []
        for (key, sem, val, peng) in sorted(deps, key=lambda d: (d[0], d[2])):
            if peng == "pe" and eng == "pe":
                continue
            if self.waited[eng].get(key, 0) >= val:
                continue
            self.waited[eng][key] = val
            out.append((sem, val))
        return out

    def op(self, eng, fn, reads=(), writes=()):
        deps = self._collect(reads, writes)
        waits = self._waits(eng, deps)
        self.cnt[eng] += 1
        tok = ("e_" + eng, self.esem[eng], self.cnt[eng], eng)
        self.ops[eng].append((waits, fn, (self.esem[eng], 1)))
        self._update(reads, writes, tok)

    def dma(self, q, fn, reads=(), writes=()):
        deps = self._collect(reads, writes)
        i = self.dnext[q]
        self.dnext[q] = (i + 1) % NDS
        sem = self.dsem[q][i]
        key = "d_%s%d" % (q, i)
        prev = self.duse[q][i]
        if prev > 0:
            deps.add((key, sem, 16 * prev, "dma"))
        self.duse[q][i] = prev + 1
        waits = self._waits(q, deps)
        tok = (key, sem, 16 * (prev + 1), "dma")
        self.ops[q].append((waits, fn, (sem, 16)))
        self._update(reads, writes, tok)

    def barrier(self):
        toks = []
        for e in ["pe", "act", "dve", "pool"]:
            if self.cnt[e] > 0:
                toks.append(("e_" + e, self.esem[e], self.cnt[e], "bar"))
        for q in ["sp", "pool"]:
            for i in range(NDS):
                if self.duse[q][i] > 0:
                    toks.append(("d_%s%d" % (q, i), self.dsem[q][i], 16 * self.duse[q][i], "bar"))
        for e in self.ENG:
            w = self._waits(e, toks)
            if w:
                self.ops[e].append((w, None, None))

    def emit(self, block):
        def run(e):
            def body(eng):
                for (waits, fn, inc) in self.ops[e]:
                    for (sem, val) in waits:
                        eng.wait_ge(sem, val)
                    if fn is not None:
                        ins = fn(eng)
                        ins.then_inc(inc[0], inc[1])
            return body
        block.tensor(run("pe"))
        block.scalar(run("act"))
        block.vector(run("dve"))
        block.gpsimd(run("pool"))
        block.sync(run("sp"))


def build_program():
    nc = bass.Bass("TRN2", target_bir_lowering=False)

    def din(name, shape, dt=F32):
        return nc.dram_tensor(name, list(shape), dt, kind="ExternalInput").ap()

    xp_full = din("xp_full", [SEQ_P, D])
    xs_full = din("xs_full", [SEQ_S, D])
    x_own = din("x_own", [EXT, D])
    w_fout = din("w_fout", [D, D])
    w_qkv = din("w_qkv", [D, 2560])
    w_o = din("w_o", [D, D])
    w_up = din("w_up", [2, D, DFF])
    w_down = din("w_down", [2, DFF, D])
    gains = din("gains", [128, 4, DC])
    qk_gain = din("qk_gain", [128, 2])
    sinkb = din("sinkb", [128, 32])
    attn_scale = din("attn_scale", [128, 1])
    halfmask = din("halfmask", [128, 2])
    masks = din("masks", [128, 3, 384], BF16)
    ropecs = din("ropecs", [128, 2, EXT])
    ctab = din("ctab", [128, 2, 2, 256])
    wdft_p = din("wdft_p", [N2_P, 2 * N2_P], BF16)
    wdft_s = din("wdft_s", [N2_S, 2 * N2_S], BF16)
    g_p = din("g_p", [128, N2_P, 3 * K1_P], BF16)
    g_s = din("g_s", [128, N2_S, 3 * K1_S], BF16)
    ident_f = din("ident_f", [128, 128])
    ident_b = din("ident_b", [128, 128], BF16)
    ones_d = din("ones_d", [128, 128], BF16)
    ones_h = din("ones_h", [128, 128], BF16)
    rot_t = din("rot_t", [128, 128], BF16)
    y_own = nc.dram_tensor("y_own", [OWN, D], F32, kind="ExternalOutput").ap()

    fT_s = nc.dram_tensor("fT_s", [D, EXT], BF16, kind="Internal").ap()
    x2T_s = nc.dram_tensor("x2T_s", [D, EXT], F32, kind="Internal").ap()
    qT_s = nc.dram_tensor("qT_s", [D, EXT], BF16, kind="Internal").ap()
    kT_s = nc.dram_tensor("kT_s", [4, 2, 128, EXT], BF16, kind="Internal").ap()
    v_s = nc.dram_tensor("v_s", [EXT, 256], BF16, kind="Internal").ap()
    wfout_b = nc.dram_tensor("wfout_b", [4, 128, 16, 512], BF16, kind="Internal").ap()
    wq_b = nc.dram_tensor("wq_b", [4, 128, 16, 512], BF16, kind="Internal").ap()
    wk_b = nc.dram_tensor("wk_b", [1, 128, 16, 512], BF16, kind="Internal").ap()
    wo_b = nc.dram_tensor("wo_b", [4, 128, 16, 512], BF16, kind="Internal").ap()
    wup_b = nc.dram_tensor("wup_b", [2, 16, 128, 16, 512], BF16, kind="Internal").ap()
    wdown_b = nc.dram_tensor("wdown_b", [2, 2, 8, 128, 32, 256], BF16, kind="Internal").ap()

    with contextlib.ExitStack() as stack:
        S = Sched(nc, stack)

        def sb(name, shape, dt):
            return stack.enter_context(nc.sbuf_tensor(name, list(shape), dt))

        ps = [stack.enter_context(nc.psum_tensor("ps%d" % i, [128, 512], F32)) for i in range(8)]
        psr = [Res("ps%d" % i) for i in range(8)]

        gains_t = sb("gains_t", [128, 4, DC], F32)
        qkg_t = sb("qkg_t", [128, 2], F32)
        sink_t = sb("sink_t", [128, 32], F32)
        asc_t = sb("asc_t", [128, 1], F32)
        hm_t = sb("hm_t", [128, 2], F32)
        idf_t = sb("idf_t", [128, 128], F32)
        idb_t = sb("idb_t", [128, 128], BF16)
        onesd_t = sb("onesd_t", [128, 128], BF16)
        onesh_t = sb("onesh_t", [128, 128], BF16)
        rot_tt = sb("rot_tt", [128, 128], BF16)
        rstd_seq = sb("rstd_seq", [128, 2, 128], F32)
        CONST = Res("const")

        def ld(dst, src):
            S.dma("sp", lambda e: e.dma_start(out=dst, in_=src), writes=[CONST])
        ld(gains_t[:], gains)
        ld(qkg_t[:], qk_gain)
        ld(sink_t[:], sinkb)
        ld(asc_t[:], attn_scale)
        ld(hm_t[:], halfmask)
        ld(idf_t[:], ident_f)
        ld(idb_t[:], ident_b)
        ld(onesd_t[:], ones_d)
        ld(onesh_t[:], ones_h)
        ld(rot_tt[:], rot_t)
        S.barrier()
        S.op("dve", lambda e: e.tensor_tensor(out=qkg_t[:, 0:1], in0=qkg_t[:, 0:1], in1=asc_t[:], op=ALU.mult),
             reads=[CONST], writes=[CONST])
        S.barrier()
        CONST.const = True
        CONST.w = None
        CONST.rs = []

        conv = []

        def cv(dst, src):
            conv.append(lambda: S.dma("pool", (lambda e: e.dma_start(out=dst, in_=src)), writes=[CONVW]))
        CONVW = Res("convw")
        for mb in range(4):
            cv(wfout_b[mb], w_fout[:, mb * 512:(mb + 1) * 512].rearrange("(kc p) m -> p kc m", p=128))

        def cv_mlp(li):
            for mb in range(16):
                cv(wup_b[li, mb], w_up[li][:, mb * 512:(mb + 1) * 512].rearrange("(kc p) m -> p kc m", p=128))
            for half in range(2):
                for mb in range(8):
                    cv(wdown_b[li, half, mb], w_down[li][half * 4096:(half + 1) * 4096, mb * 256:(mb + 1) * 256].rearrange("(kc p) m -> p kc m", p=128))
        cv_mlp(0)
        for mb in range(4):
            cv(wq_b[mb], w_qkv[:, mb * 512:(mb + 1) * 512].rearrange("(kc p) m -> p kc m", p=128))
        for kv in range(4):
            for dup in range(2):
                cv(wk_b[0][:, :, kv * 128 + dup * 64:kv * 128 + dup * 64 + 64],
                   w_qkv[:, 2048 + kv * 64:2048 + (kv + 1) * 64].rearrange("(kc p) d -> p kc d", p=128))
        n_conv0 = len(conv)
        for mb in range(4):
            cv(wo_b[mb], w_o[:, mb * 512:(mb + 1) * 512].rearrange("(kc p) m -> p kc m", p=128))
        cv_mlp(1)
        n_conv1 = len(conv) - n_conv0

        seqs = [
            dict(x=xp_full, N2=N2_P, K1=K1_P, G=g_p, W=wdft_p, eoff=0, ext=EXT_P, scale=1.0 / math.sqrt(SEQ_P * 256.0), si=0),
            dict(x=xs_full, N2=N2_S, K1=K1_S, G=g_s, W=wdft_s, eoff=EXT_P, ext=EXT_S, scale=1.0 / math.sqrt(SEQ_S * 256.0), si=1),
        ]

        with contextlib.ExitStack() as st0:
            def sb0(name, shape, dt):
                return st0.enter_context(nc.sbuf_tensor(name, list(shape), dt))
            xin = [sb0("xin%d" % i, [128, 2, D], F32) for i in range(3)]
            xin_r = [Res("xin%d" % i) for i in range(3)]
            junk = sb0("junk", [128, D], BF16)
            junk2 = sb0("junk2", [128, D], BF16)
            ss = sb0("ss", [128, 2, 128], F32)
            lnt = sb0("lnt", [128, 128], F32)
            ss_r = Res("ss")
            lnt_r = Res("lnt")
            rstd_r = Res("rstd")
            rot_r = {"act": [Res("f0a%d" % i) for i in range(4)], "dve": [Res("f0d%d" % i) for i in range(4)]}
            S.op("dve", lambda e: e.memset(ss[:], 0.0), writes=[ss_r])
            S.barrier()
            it = 0
            k_ = {"act": 0, "dve": 0}
            for sq in seqs:
                N2 = sq["N2"]
                xv = sq["x"].rearrange("(a n) c -> a n c", n=128)
                for n1 in range(0, 128, 2):
                    b = it % 3
                    it += 1
                    S.dma("sp", (lambda e, b=b, n1=n1, xv=xv, N2=N2: e.dma_start(out=xin[b][0:N2, :, :], in_=xv[:, n1:n1 + 2, :])),
                          writes=[xin_r[b]])
                    for t in range(2):
                        eng = "act" if t == 0 else "dve"
                        rr = rot_r[eng][k_[eng] % 4]
                        k_[eng] += 1
                        if eng == "act":
                            S.op("act", (lambda e, b=b, t=t, n1=n1, N2=N2, si=sq["si"]: e.activation(
                                out=junk[0:N2, :], in_=xin[b][0:N2, t, :], func=ACT.Square,
                                accum_out=ss[0:N2, si, n1 + t:n1 + t + 1])),
                                reads=[xin_r[b]], writes=[rr])
                        else:
                            S.op("dve", (lambda e, b=b, t=t, n1=n1, N2=N2, si=sq["si"]: e.scalar_tensor_tensor(
                                out=junk2[0:N2, :], in0=xin[b][0:N2, t, :], scalar=1.0, in1=xin[b][0:N2, t, :],
                                op0=ALU.mult, op1=ALU.mult, accum_out=ss[0:N2, si, n1 + t:n1 + t + 1])),
                                reads=[xin_r[b]], writes=[rr])
            S.barrier()
            for sq in seqs:
                si = sq["si"]
                S.op("act", (lambda e, si=si: e.activation(out=lnt[:], in_=ss[:, si, :], func=ACT.Ln, bias=EPS, scale=1.0 / D)),
                     reads=[ss_r], writes=[lnt_r])
                S.op("act", (lambda e, si=si: e.activation(out=rstd_seq[:, si, :], in_=lnt[:], func=ACT.Exp, scale=-0.5)),
                     reads=[lnt_r], writes=[rstd_r])
            S.barrier()

        with contextlib.ExitStack() as st1:
            def sb1(name, shape, dt):
                return st1.enter_context(nc.sbuf_tensor(name, list(shape), dt))
            Yb = sb1("Yb", [128, 128, 128], BF16)
            U = sb1("U", [128, 128 * 2 * 128], BF16)
            ZT = sb1("ZT", [128, 2, 2, EXT_P], BF16)
            Gt = sb1("Gt", [128, N2_P * 3 * K1_P], BF16)
            Wt = sb1("Wt", [128, 256], BF16)
            CG = sb1("CG", [128, DC, 2, 256], BF16)
            ctab_t = sb1("ctab_t", [128, 2, 2, 256], F32)
            fst = [sb1("fst%d" % i, [128, 512], BF16) for i in range(4)]
            ystg = [sb1("ystg%d" % i, [128, 16, 128], F32) for i in range(3)]
            ystg_r = [Res("ystg%d" % i) for i in range(3)]
            stq = [0]
            Yb_r = [Res("Yb%d" % i) for i in range(8)]
            U_r, Gt_r, Wt_r, CG_r = Res("U"), Res("Gt"), Res("Wt"), Res("CG")
            ZT_r = [Res("ZT0"), Res("ZT1")]
            fst_r = [Res("fst%d" % i) for i in range(4)]
            ctab_r = Res("ctab")
            fT_r = Res("fT_s")

            S.dma("sp", lambda e: e.dma_start(out=ctab_t[:], in_=ctab), writes=[ctab_r])
            for ch in range(DC):
                kc = ch % 2
                for cs in range(2):
                    S.op("dve", (lambda e, ch=ch, kc=kc, cs=cs: e.tensor_scalar(
                        out=CG[:, ch, cs, :], in0=ctab_t[:, kc, cs, :], scalar1=gains_t[:, 0, ch:ch + 1], scalar2=None, op0=ALU.mult)),
                        reads=[ctab_r], writes=[CG_r])

            cnt1 = dict(evq=0, fq=0)

            def f1_views(sq):
                N2, K1 = sq["N2"], sq["K1"]
                xv = sq["x"].rearrange("(a n) c -> a n c", n=128)
                Gv = Gt[:, 0:N2 * 3 * K1].rearrange("p (k c) -> p k c", c=3 * K1)
                Uv = U[:, 0:N2 * 256].rearrange("p (c r k) -> p c r k", r=2, k=N2)
                return xv, Gv, Uv

            def f1_prep(sq, ch):
                N2, si = sq["N2"], sq["si"]
                xv, Gv, Uv = f1_views(sq)
                for _ in range(2):
                    if len(conv) > n_conv1:
                        conv.pop(0)()
                for pc in range(8):
                    sg = stq[0] % 3
                    stq[0] += 1
                    S.dma("sp", (lambda e, pc=pc, sg=sg: e.dma_start(
                        out=ystg[sg][0:N2, :, :], in_=xv[:, pc * 16:(pc + 1) * 16, ch * 128:(ch + 1) * 128])),
                        writes=[ystg_r[sg]])
                    eng = "pool" if (si == 0 or pc in (0, 3, 6)) else "dve"
                    S.op(eng, (lambda e, pc=pc, sg=sg: e.tensor_tensor(
                        out=Yb[0:N2, pc * 16:(pc + 1) * 16, :], in0=ystg[sg][0:N2, :, :],
                        in1=rstd_seq[0:N2, si, pc * 16:(pc + 1) * 16].unsqueeze(2).to_broadcast([N2, 16, 128]), op=ALU.mult)),
                        reads=[ystg_r[sg]], writes=[Yb_r[pc]])

            def f1_s1(sq, ch):
                N2 = sq["N2"]
                xv, Gv, Uv = f1_views(sq)
                nb1 = 512 // (2 * N2)
                if ch == 0:
                    S.dma("sp", (lambda e: e.dma_start(out=Wt[0:N2, 0:2 * N2], in_=sq["W"])), writes=[Wt_r])
                for c0 in range(0, 128, nb1):
                    bk = (c0 // nb1) % 4

                    def s1(e, c0=c0, bk=bk):
                        ins = None
                        for i in range(nb1):
                            ins = e.matmul(ps[bk][:, i * 2 * N2:(i + 1) * 2 * N2], lhsT=Yb[0:N2, :, c0 + i],
                                           rhs=Wt[0:N2, 0:2 * N2], start=True, stop=True)
                        return ins
                    S.op("pe", s1, reads=Yb_r + [Wt_r], writes=[psr[bk]])
                    src = ps[bk][:, 0:nb1 * 2 * N2].rearrange("p (c r k) -> p c r k", r=2, k=N2)
                    dst = Uv[:, c0:c0 + nb1, :, :]
                    eng = "act" if cnt1["evq"] % 2 == 0 else "dve"
                    cnt1["evq"] += 1
                    if eng == "act":
                        S.op("act", (lambda e, src=src, dst=dst: e.activation(out=dst, in_=src, func=ACT.Copy)),
                             reads=[psr[bk]], writes=[U_r])
                    else:
                        S.op("dve", (lambda e, src=src, dst=dst: e.tensor_copy(out=dst, in_=src)),
                             reads=[psr[bk]], writes=[U_r])

            def f1_s2(sq, ch):
                N2, K1, ext = sq["N2"], sq["K1"], sq["ext"]
                xv, Gv, Uv = f1_views(sq)
                ci = ch % 2
                nk = 512 // (2 * K1)
                if ch == 0:
                    S.dma("sp", (lambda e: e.dma_start(out=Gv, in_=sq["G"])), writes=[Gt_r])
                ZTv = ZT[:, ci, :, 0:ext].rearrange("p r (a k) -> p r a k", k=N2)
                for k0 in range(0, N2, nk):
                    n = min(nk, N2 - k0)
                    bk = 4 + (k0 // nk) % 4

                    def s2(e, k0=k0, n=n, bk=bk):
                        ins = None
                        for i in range(n):
                            o = ps[bk][:, i * 2 * K1:(i + 1) * 2 * K1]
                            e.matmul(o, lhsT=Uv[:, :, 0, k0 + i], rhs=Gv[:, k0 + i, K1:3 * K1], start=True, stop=False)
                            ins = e.matmul(o, lhsT=Uv[:, :, 1, k0 + i], rhs=Gv[:, k0 + i, 0:2 * K1], start=False, stop=True)
                        return ins
                    S.op("pe", s2, reads=[U_r, Gt_r], writes=[psr[bk]])
                    src = ps[bk][:, 0:n * 2 * K1].rearrange("p (k r a) -> p k r a", r=2, a=K1)
                    dst = ZTv[:, :, :, k0:k0 + n].rearrange("p r a k -> p k r a")
                    eng = "act" if cnt1["evq"] % 2 == 0 else "dve"
                    cnt1["evq"] += 1
                    if eng == "act":
                        S.op("act", (lambda e, src=src, dst=dst: e.activation(out=dst, in_=src, func=ACT.Copy)),
                             reads=[psr[bk]], writes=[ZT_r[ci]])
                    else:
                        S.op("dve", (lambda e, src=src, dst=dst: e.tensor_copy(out=dst, in_=src)),
                             reads=[psr[bk]], writes=[ZT_r[ci]])
                if ci == 1:
                    g = ch // 2
                    for t0 in range(0, ext, 512):
                        n = min(512, ext - t0)
                        for mc in range(2):
                            bk = (cnt1["fq"] % 4)

                            def cd(e, mc=mc, t0=t0, n=n, bk=bk):
                                ins = None
                                idx = 0
                                for kc in range(2):
                                    for cs in range(2):
                                        ins = e.matmul(ps[bk][:, 0:n], lhsT=CG[:, g * 2 + kc, cs, mc * 128:(mc + 1) * 128],
                                                       rhs=ZT[:, kc, cs, t0:t0 + n], start=(idx == 0), stop=(idx == 3))
                                        idx += 1
                                return ins
                            S.op("pe", cd, reads=[CG_r] + ZT_r, writes=[psr[bk]])
                            fb = cnt1["fq"] % 4
                            cnt1["fq"] += 1
                            S.op("act", (lambda e, fb=fb, bk=bk, n=n: e.activation(
                                out=fst[fb][:, 0:n], in_=ps[bk][:, 0:n], func=ACT.Copy, scale=sq["scale"])),
                                reads=[psr[bk]], writes=[fst_r[fb]])
                            row = (g * 2 + mc) * 128
                            S.dma("sp", (lambda e, fb=fb, n=n, row=row, col=sq["eoff"] + t0: e.dma_start(
                                out=fT_s[row:row + 128, col:col + n], in_=fst[fb][:, 0:n])),
                                reads=[fst_r[fb]], writes=[fT_r])

            items = [(sq, ch) for sq in seqs for ch in range(DC)]
            f1_prep(*items[0])
            for ii, it in enumerate(items):
                f1_s1(*it)
                if ii + 1 < len(items):
                    f1_prep(*items[ii + 1])
                f1_s2(*it)
            while len(conv) > n_conv1:
                conv.pop(0)()
            S.barrier()

        with contextlib.ExitStack() as st2:
            def sb2(name, shape, dt):
                return st2.enter_context(nc.sbuf_tensor(name, list(shape), dt))
            xT = sb2("xT", [128, DC, 512], F32)
            hT = sb2("hT", [128, DC, 512], BF16)
            aT = sb2("aT", [128, 32, 512], BF16)
            NW = 3
            wbuf = [sb2("wbuf%d" % i, [128, 8192], BF16) for i in range(NW)]
            NT = 4
            tmpf = [sb2("tmpf%d" % i, [128, 512], F32) for i in range(NT)]
            tmpg = [sb2("tmpg%d" % i, [128, 512], F32) for i in range(NT)]
            tmpb = [sb2("tmpb%d" % i, [128, 512], BF16) for i in range(NT)]
            rstd_t = sb2("rstd_t", [128, 512], F32)
            rstdT = sb2("rstdT", [128, 4], F32)
            rstdT_r = Res("rstdT")
            xtok = [sb2("xtok%d" % i, [128, D], F32) for i in range(2)]
            kst = sb2("kst", [128, 4, 2, 768], BF16)
            vst = sb2("vst", [128, 6, 256], BF16)

            xT_r = [Res("xT%d" % i) for i in range(DC)]
            hT_r = [Res("hT%d" % i) for i in range(DC)]
            aT_r = [Res("aT%d" % i) for i in range(32)]
            wbuf_r = [Res("wbuf%d" % i) for i in range(NW)]
            wv_r = Res("wv")
            tmpf_r = [Res("tmpf%d" % i) for i in range(NT)]
            tmpg_r = [Res("tmpg%d" % i) for i in range(NT)]
            tmpb_r = [Res("tmpb%d" % i) for i in range(NT)]
            rstd_r2, lnt2_r, cs_r = Res("rstd_t"), Res("lnt2"), Res("cs_t")
            xtok_r = [Res("xtok0"), Res("xtok1")]
            kst_r, vst_r, vpad_r = Res("kst"), Res("vst"), Res("vpad")
            pbf_r = [Res("pbf%d" % i) for i in range(3)]
            dgb_r = [Res("dgb%d" % i) for i in range(3)]
            negm_r = [Res("negm%d" % i) for i in range(3)]
            rsum_r = [Res("rsum%d" % i) for i in range(3)]
            rden_r = [Res("rden%d" % i) for i in range(3)]
            pTs_r = [Res("pTs0"), Res("pTs1")]
            stat_r = [Res("stat0"), Res("stat1")]
            mask_r = Res("mask")
            fT_r2, x2T_r, qT_r, kT_r, v_r, y_r = Res("fT2"), Res("x2T"), Res("qTs"), Res("kTs"), Res("vs"), Res("y")

            state = dict(w=0, lb=0, ev=0, mb=0)

            def linear(KC, MBW, blocks, rhs, rhs_res, evac, n, mc0=0, tail=None):
                nmi = MBW // 128
                pend = []
                for mb, blk in enumerate(blocks):
                    wi = state["w"] % NW
                    state["w"] += 1
                    wv = wbuf[wi][:, 0:KC * MBW].rearrange("p (k m) -> p k m", m=MBW)
                    S.dma("sp", (lambda e, wv=wv, blk=blk: e.dma_start(out=wv, in_=blk)), writes=[wbuf_r[wi]])
                    for mi in range(nmi):
                        bk = state["lb"] % 4
                        state["lb"] += 1

                        def mm(e, wv=wv, bk=bk, mi=mi):
                            ins = None
                            for kc in range(KC):
                                ins = e.matmul(ps[bk][:, 0:n], lhsT=wv[:, kc, mi * 128:(mi + 1) * 128], rhs=rhs(kc), start=(kc == 0), stop=(kc == KC - 1))
                            return ins
                        S.op("pe", mm, reads=[wbuf_r[wi]] + list(rhs_res), writes=[psr[bk]])
                        st = evac(mc0 + mb * nmi + mi, bk)
                        if st:
                            pend.append(list(st))
                        for lag, stages in enumerate(reversed(pend[:-1] if st else pend)):
                            pass
                        for stages in pend[:-1] if st else pend:
                            if stages:
                                stages.pop(0)()
                        while pend and not pend[0]:
                            pend.pop(0)
                if tail is not None:
                    tail(pend)
                while pend:
                    for stages in pend:
                        if stages:
                            stages.pop(0)()
                    while pend and not pend[0]:
                        pend.pop(0)

            def rmsnorm(gi, n, need_T=False):
                for dc in range(DC):
                    S.op("dve", (lambda e, dc=dc: e.tensor_scalar(out=hT[:, dc, 0:n], in0=xT[:, dc, 0:n], scalar1=gains_t[:, gi, dc:dc + 1],
                                                                  scalar2=None, op0=ALU.mult)),
                         reads=[xT_r[dc]], writes=[hT_r[dc]])
                for dc in range(DC):
                    if dc % 2 == 0:
                        S.op("act", (lambda e, dc=dc: e.activation(out=aT[:, dc, 0:n], in_=xT[:, dc, 0:n], func=ACT.Square)),
                             reads=[xT_r[dc]], writes=[aT_r[dc]])
                    else:
                        S.op("pool", (lambda e, dc=dc: e.tensor_tensor(out=aT[:, dc, 0:n], in0=xT[:, dc, 0:n], in1=xT[:, dc, 0:n], op=ALU.mult)),
                             reads=[xT_r[dc]], writes=[aT_r[dc]])
                bk = 4 + state["mb"] % 4
                state["mb"] += 1

                def mm(e, bk=bk):
                    ins = None
                    for dc in range(DC):
                        ins = e.matmul(ps[bk][:, 0:n], lhsT=onesd_t[:], rhs=aT[:, dc, 0:n], start=(dc == 0), stop=(dc == DC - 1))
                    return ins
                S.op("pe", mm, reads=aT_r[0:DC], writes=[psr[bk]])
                S.op("act", (lambda e, bk=bk: e.activation(out=rstd_t[:, 0:n], in_=ps[bk][:, 0:n], func=ACT.Ln, bias=EPS, scale=1.0)),
                     reads=[psr[bk]], writes=[rstd_r2])
                S.op("act", (lambda e: e.activation(out=rstd_t[:, 0:n], in_=rstd_t[:, 0:n], func=ACT.Exp, scale=-0.5)),
                     reads=[rstd_r2], writes=[rstd_r2])
                if need_T:
                    nblk = n // 128
                    b2 = 4 + state["mb"] % 4
                    state["mb"] += 1

                    def tt(e, b2=b2):
                        ins = None
                        for blk in range(nblk):
                            ins = e.matmul(ps[b2][:, blk:blk + 1], lhsT=rstd_t[0:1, blk * 128:(blk + 1) * 128], rhs=idf_t[0:1, 0:1],
                                           start=True, stop=True)
                        return ins
                    S.op("pe", tt, reads=[rstd_r2], writes=[psr[b2]])
                    S.op("act", (lambda e, b2=b2: e.activation(out=rstdT[:, 0:nblk], in_=ps[b2][:, 0:nblk], func=ACT.Copy)),
                         reads=[psr[b2]], writes=[rstdT_r])

            def add_evac(n):
                def f(mc, bk):
                    S.op("dve", (lambda e: e.tensor_tensor(out=xT[:, mc, 0:n], in0=ps[bk][:, 0:n], in1=xT[:, mc, 0:n], op=ALU.add)),
                         reads=[psr[bk], xT_r[mc]], writes=[xT_r[mc]])
                return f

            def mlp(li, n):
                for half in range(2):
                    def up_evac(mc, bk):
                        i = state["ev"] % NT
                        state["ev"] += 1
                        S.op("act", (lambda e: e.activation(out=tmpf[i][:, 0:n], in_=ps[bk][:, 0:n], func=ACT.Relu)),
                             reads=[psr[bk]], writes=[tmpf_r[i]])
                        S.op("dve", (lambda e: e.tensor_tensor(out=tmpf[i][:, 0:n], in0=tmpf[i][:, 0:n], in1=rstd_t[:, 0:n], op=ALU.mult)),
                             reads=[tmpf_r[i], rstd_r2], writes=[tmpf_r[i]])
                        S.op("pool", (lambda e: e.tensor_tensor(out=aT[:, mc, 0:n], in0=tmpf[i][:, 0:n], in1=tmpf[i][:, 0:n], op=ALU.mult)),
                             reads=[tmpf_r[i]], writes=[aT_r[mc]])
                    linear(DC, 512, [wup_b[li, half * 8 + mb] for mb in range(8)], (lambda kc: hT[:, kc, 0:n]), hT_r, up_evac, n)
                    linear(32, 256, [wdown_b[li, half, mb] for mb in range(8)], (lambda kc: aT[:, kc, 0:n]), aT_r, add_evac(n), n)

            st_t1 = contextlib.ExitStack()
            wv_t = st_t1.enter_context(nc.sbuf_tensor("wv_t", [128, DC, 256], BF16))
            cs_t = st_t1.enter_context(nc.sbuf_tensor("cs_t", [128, 2, 512], F32))
            S.dma("pool", lambda e: e.dma_start(out=wv_t[:], in_=w_qkv[:, 2304:2560].rearrange("(kc p) m -> p kc m", p=128)), writes=[wv_r])

            tiles1 = [(t * 512, 512) for t in range(EXT // 512)]
            xq = 0
            def x_load(r0, xb_):
                S.dma("sp", (lambda e: e.dma_start(out=xtok[xb_][:], in_=x_own[r0:r0 + 128, :])), writes=[xtok_r[xb_]])

            def t1_tile(e0, n, xq, prefetched, next_e0):
                    nblk = n // 128
                    for blk in range(nblk):
                        xb_ = xq % 2
                        xq += 1
                        if not (prefetched and blk < 2):
                            x_load(e0 + blk * 128, xb_)
                        for d0 in range(0, DC, 4):
                            bk = 4 + state["mb"] % 4
                            state["mb"] += 1

                            def tr(e, xb_=xb_, d0=d0, bk=bk):
                                ins = None
                                for i in range(4):
                                    ins = e.matmul(ps[bk][:, i * 128:(i + 1) * 128], lhsT=xtok[xb_][:, (d0 + i) * 128:(d0 + i + 1) * 128],
                                                   rhs=idf_t[:], start=True, stop=True)
                                return ins
                            S.op("pe", tr, reads=[xtok_r[xb_]], writes=[psr[bk]])
                            eng = "act" if (d0 // 4) % 2 == 0 else "dve"
                            src = ps[bk][:, 0:512].rearrange("p (c t) -> p c t", t=128)
                            dst = xT[:, d0:d0 + 4, blk * 128:(blk + 1) * 128]
                            if eng == "act":
                                S.op("act", (lambda e, src=src, dst=dst: e.activation(out=dst, in_=src, func=ACT.Copy)),
                                     reads=[psr[bk]], writes=xT_r[d0:d0 + 4])
                            else:
                                S.op("dve", (lambda e, src=src, dst=dst: e.tensor_copy(out=dst, in_=src)),
                                     reads=[psr[bk]], writes=xT_r[d0:d0 + 4])
                    S.dma("sp", (lambda e, e0=e0, n=n: e.dma_start(out=hT[:, :, 0:n], in_=fT_s[:, e0:e0 + n].rearrange("(kc p) t -> p kc t", p=128))),
                          reads=[fT_r2], writes=hT_r)
                    linear(DC, 512, [wfout_b[mb] for mb in range(4)], (lambda kc: hT[:, kc, 0:n]), hT_r, add_evac(n), n)
                    rmsnorm(1, n)
                    mlp(0, n)
                    if next_e0 is not None:
                        x_load(next_e0, xq % 2)
                        x_load(next_e0 + 128, (xq + 1) % 2)
                    S.dma("sp", (lambda e, e0=e0, n=n: e.dma_start(out=x2T_s[:, e0:e0 + n].rearrange("(kc p) t -> p kc t", p=128), in_=xT[:, :, 0:n])),
                          reads=xT_r, writes=[x2T_r])
                    rmsnorm(2, n, need_T=True)
                    S.dma("sp", (lambda e, e0=e0, n=n: e.dma_start(out=cs_t[:, :, 0:n], in_=ropecs[:, :, e0:e0 + n])), writes=[cs_r])

                    def qk_wsrc(mc):
                        if mc < 16:
                            return wsrc_std(w_qkv, DC)(mc)
                        kv = mc - 16
                        src = w_qkv[:, 2048 + kv * 64:2048 + (kv + 1) * 64].rearrange("(kc p) m -> p kc m", p=128)
                        return [((lambda wt: wt[:, 0:16, 0:64]), src), ((lambda wt: wt[:, 0:16, 64:128]), src)]

                    def qk_evac(mc, bk):
                        i = state["ev"] % NT
                        state["ev"] += 1
                        gcol = 0 if mc < 16 else 1
                        S.op("dve", (lambda e: e.tensor_tensor(out=tmpf[i][:, 0:n], in0=ps[bk][:, 0:n], in1=rstd_t[:, 0:n], op=ALU.mult)),
                             reads=[psr[bk], rstd_r2], writes=[tmpf_r[i]])
                        S.op("pool", (lambda e: e.tensor_tensor(out=tmpb[i][:, 0:n], in0=tmpf[i][:, 0:n], in1=tmpf[i][:, 0:n], op=ALU.mult)),
                             reads=[tmpf_r[i]], writes=[tmpb_r[i]])
                        loc = {}

                        def stage2():
                            b2 = 4 + state["mb"] % 4
                            state["mb"] += 1
                            S.op("pe", (lambda e: e.matmul(ps[b2][:, 0:n], lhsT=onesh_t[:], rhs=tmpb[i][:, 0:n], start=True, stop=True)),
                                 reads=[tmpb_r[i]], writes=[psr[b2]])
                            S.op("act", (lambda e: e.activation(out=tmpg[i][:, 0:n], in_=ps[b2][:, 0:n], func=ACT.Ln, bias=EPS, scale=1.0)),
                                 reads=[psr[b2]], writes=[tmpg_r[i]])
                            S.op("act", (lambda e: e.activation(out=tmpg[i][:, 0:n], in_=tmpg[i][:, 0:n], func=ACT.Exp, scale=-0.5)),
                                 reads=[tmpg_r[i]], writes=[tmpg_r[i]])
                            S.op("dve", (lambda e: e.scalar_tensor_tensor(out=tmpb[i][:, 0:n], in0=tmpf[i][:, 0:n], scalar=qkg_t[:, gcol:gcol + 1],
                                                                          in1=tmpg[i][:, 0:n], op0=ALU.mult, op1=ALU.mult)),
                                 reads=[tmpf_r[i], tmpg_r[i], tmpb_r[i]], writes=[tmpb_r[i]])

                        def stage3():
                            b3 = 4 + state["mb"] % 4
                            state["mb"] += 1
                            loc["b3"] = b3
                            S.op("pe", (lambda e: e.matmul(ps[b3][:, 0:n], lhsT=rot_tt[:], rhs=tmpb[i][:, 0:n], start=True, stop=True)),
                                 reads=[tmpb_r[i]], writes=[psr[b3]])
                            S.op("pool", (lambda e: e.tensor_tensor(out=tmpf[i][:, 0:n], in0=tmpb[i][:, 0:n], in1=cs_t[:, 0, 0:n], op=ALU.mult)),
                                 reads=[tmpb_r[i], cs_r], writes=[tmpf_r[i]])

                        def stage4():
                            b3 = loc["b3"]
                            S.op("dve", (lambda e: e.tensor_tensor(out=tmpg[i][:, 0:n], in0=ps[b3][:, 0:n], in1=cs_t[:, 1, 0:n], op=ALU.mult)),
                                 reads=[psr[b3], cs_r], writes=[tmpg_r[i]])
                            if mc < 16:
                                S.op("pool", (lambda e: e.tensor_tensor(out=aT[:, mc, 0:n], in0=tmpf[i][:, 0:n], in1=tmpg[i][:, 0:n], op=ALU.add)),
                                     reads=[tmpf_r[i], tmpg_r[i]], writes=[aT_r[mc]])
                            else:
                                kv = mc - 16
                                S.op("pool", (lambda e: e.tensor_tensor(out=tmpf[i][:, 0:n], in0=tmpf[i][:, 0:n], in1=tmpg[i][:, 0:n], op=ALU.add)),
                                     reads=[tmpf_r[i], tmpg_r[i]], writes=[tmpf_r[i]])
                                for eo in range(2):
                                    S.op("dve", (lambda e, eo=eo: e.tensor_scalar(out=kst[:, kv, eo, 0:n], in0=tmpf[i][:, 0:n], scalar1=hm_t[:, eo:eo + 1],
                                                                                  scalar2=None, op0=ALU.mult)),
                                         reads=[tmpf_r[i]], writes=[kst_r])
                        return [stage2, stage3, stage4]
                    def v_tail(pend):
                        for blk in range(nblk):
                            bk = 4 + state["mb"] % 4
                            state["mb"] += 1

                            def vm(e, blk=blk, bk=bk):
                                ins = None
                                for kc in range(DC):
                                    ins = e.matmul(ps[bk][:, 0:256], lhsT=hT[:, kc, blk * 128:(blk + 1) * 128], rhs=wv_t[:, kc, :],
                                                   start=(kc == 0), stop=(kc == DC - 1))
                                return ins
                            S.op("pe", vm, reads=hT_r + [wv_r], writes=[psr[bk]])
                            S.op("act", (lambda e, blk=blk, bk=bk: e.activation(out=vst[:, blk, :], in_=ps[bk][:, 0:256], func=ACT.Copy,
                                                                                scale=rstdT[:, blk:blk + 1])),
                                 reads=[psr[bk], rstdT_r], writes=[vst_r])
                            for stages in pend:
                                if stages:
                                    stages.pop(0)()
                            while pend and not pend[0]:
                                pend.pop(0)
                    linear(DC, 512, [wq_b[mb] for mb in range(4)] + [wk_b[0]], (lambda kc: hT[:, kc, 0:n]), hT_r, qk_evac, n, tail=v_tail)
                    S.dma("sp", (lambda e, e0=e0, n=n: e.dma_start(out=qT_s[:, e0:e0 + n].rearrange("(kc p) t -> p kc t", p=128), in_=aT[:, 0:16, 0:n])),
                          reads=aT_r[0:16], writes=[qT_r])
                    S.dma("sp", (lambda e, e0=e0, n=n: e.dma_start(out=kT_s[:, :, :, e0:e0 + n].rearrange("a b p t -> p a b t"), in_=kst[:, :, :, 0:n])),
                          reads=[kst_r], writes=[kT_r])
                    S.dma("sp", (lambda e, e0=e0, n=n, nblk=nblk: e.dma_start(out=v_s[e0:e0 + n, :].rearrange("(b p) m -> p b m", p=128), in_=vst[:, 0:nblk, :])),
                          reads=[vst_r], writes=[v_r])
                    return xq
            for ti, (e0, n) in enumerate(tiles1):
                for _ in range(4):
                    if conv:
                        conv.pop(0)()
                nxt = tiles1[ti + 1][0] if ti + 1 < len(tiles1) else None
                xq = t1_tile(e0, n, xq, ti > 0, nxt)
            while conv:
                conv.pop(0)()
            S.barrier()
            st_t1.close()

            st_t2 = contextlib.ExitStack()

            def sb3(name, shape, dt):
                return st_t2.enter_context(nc.sbuf_tensor(name, list(shape), dt))
            vpad = sb3("vpad", [128, 6, 4, 2, 128], BF16)
            pbf = [sb3("pbf%d" % i, [128, 392], BF16) for i in range(3)]
            dgb = [sb3("dgb%d" % i, [128, 128], BF16) for i in range(3)]
            sinkb_t = sb3("sinkb_t", [128, 32], BF16)
            pTs = [sb3("pTs%d" % i, [128, 384], BF16) for i in range(2)]
            stat = [sb3("stat%d" % i, [128, 4], F32) for i in range(3)]
            mask_t = sb3("mask_t", [128, 3, 384], BF16)
            S.dma("sp", lambda e: e.dma_start(out=mask_t[:], in_=masks), writes=[mask_r])
            S.op("dve", lambda e: e.memset(vpad[:].rearrange("p a b c d -> p (a b c d)"), 0.0), writes=[vpad_r])
            S.op("dve", lambda e: e.tensor_copy(out=sinkb_t[:], in_=sink_t[:]), writes=[mask_r])
            tiles2 = [(128 + t * 512, t * 512, t == 0, t == 7) for t in range(8)] + \
                     [(EXT_P + 128 + t * 512, OWN_P + t * 512, t == 0, t == 1) for t in range(2)]
            n = 512
            aq = 0
            oq = 0
            def load_kv(e0):
                S.dma("sp", (lambda e: e.dma_start(out=kst[:], in_=kT_s[:, :, :, e0 - 128:e0 + 640].rearrange("a b p t -> p a b t"))),
                      reads=[kT_r], writes=[kst_r])
                S.dma("sp", (lambda e: e.dma_start(out=vst[:], in_=v_s[e0 - 128:e0 + 640, :].rearrange("(b p) m -> p b m", p=128))),
                      reads=[v_r], writes=[vst_r])
                for eo in range(2):
                    S.op("pool", (lambda e, eo=eo: e.tensor_copy(out=vpad[:, :, :, eo, eo * 64:(eo + 1) * 64],
                                                                 in_=vst[:].rearrange("p b (k d) -> p b k d", d=64))),
                         reads=[vst_r], writes=[vpad_r])

            def t2_tile(e0, y0, first, last, aq, oq, first_tile, next_e0):
                    if first_tile:
                        load_kv(e0)
                    S.dma("sp", (lambda e, e0=e0: e.dma_start(out=aT[:, 0:16, :], in_=qT_s[:, e0:e0 + 512].rearrange("(kc p) t -> p kc t", p=128))),
                          reads=[qT_r], writes=aT_r[0:16])
                    S.dma("sp", (lambda e, e0=e0: e.dma_start(out=xT[:], in_=x2T_s[:, e0:e0 + 512].rearrange("(kc p) t -> p kc t", p=128))),
                          reads=[x2T_r], writes=xT_r)
                    units = [(blk, hp, eo) for blk in range(4) for hp in range(16) for eo in range(2)]

                    NU = len(units)

                    def st_qk(idx):
                        blk, hp, eo = units[idx]
                        kv, h = hp // 4, 2 * hp + eo
                        bs = idx % 3
                        mi = 0 if (first and blk == 0) else (2 if (last and blk == 3) else 1)

                        def qk(e):
                            e.matmul(ps[bs][:, 384:385], lhsT=idb_t[:], rhs=sinkb_t[:, h:h + 1], start=True, stop=True)
                            e.matmul(ps[bs][:, 0:384], lhsT=idb_t[:], rhs=mask_t[:, mi, :], start=True, stop=False)
                            return e.matmul(ps[bs][:, 0:384], lhsT=aT[:, hp, blk * 128:(blk + 1) * 128],
                                            rhs=kst[:, kv, eo, blk * 128:blk * 128 + 384], start=False, stop=True)
                        S.op("pe", qk, reads=[aT_r[hp], kst_r, mask_r], writes=[psr[bs]])

                    def st_max(idx):
                        i = idx % 3
                        bs = i
                        S.op("dve", (lambda e: e.tensor_reduce(out=stat[i][:, 0:1], in_=ps[bs][:, 0:385], axis=AX.X, op=ALU.max, negate=True)),
                             reads=[psr[bs]], writes=[negm_r[i]])

                    def st_exp(idx):
                        i = idx % 3
                        bs = i
                        S.op("act", (lambda e: e.activation(out=pbf[i][:, 0:385], in_=ps[bs][:, 0:385], func=ACT.Exp, bias=stat[i][:, 0:1],
                                                            scale=1.0, accum_out=stat[i][:, 1:2])),
                             reads=[psr[bs], negm_r[i], rsum_r[i]], writes=[pbf_r[i], rsum_r[i]])

                    def st_diag(idx):
                        i = idx % 3
                        S.op("dve", (lambda e: e.reciprocal(out=stat[i][:, 2:3], in_=stat[i][:, 1:2])),
                             reads=[rsum_r[i]], writes=[rden_r[i]])
                        S.op("dve", (lambda e: e.tensor_scalar(out=dgb[i][:], in0=idb_t[:], scalar1=stat[i][:, 2:3], scalar2=None, op0=ALU.mult)),
                             reads=[rden_r[i]], writes=[dgb_r[i]])

                    def st_tp(idx):
                        i = idx % 3
                        bt = 3 + idx % 2

                        def tp(e):
                            ins = None
                            for j in range(3):
                                ins = e.matmul(ps[bt][:, j * 128:(j + 1) * 128], lhsT=pbf[i][:, j * 128:(j + 1) * 128], rhs=dgb[i][:],
                                               start=True, stop=True)
                            return ins
                        S.op("pe", tp, reads=[pbf_r[i], dgb_r[i]], writes=[psr[bt]])

                    def st_copy(idx):
                        j2 = idx % 2
                        bt = 3 + j2
                        S.op("act", (lambda e: e.activation(out=pTs[j2][:], in_=ps[bt][:, 0:384], func=ACT.Copy)),
                             reads=[psr[bt]], writes=[pTs_r[j2]])

                    def st_pv(idx):
                        blk, hp, eo = units[idx]
                        kv = hp // 4
                        j2 = idx % 2
                        bo = 5 + (idx // 2) % 2

                        def pv(e):
                            ins = None
                            for j in range(3):
                                ins = e.matmul(ps[bo][:, 0:128], lhsT=vpad[:, blk + j, kv, eo, :], rhs=pTs[j2][:, j * 128:(j + 1) * 128],
                                               start=(eo == 0 and j == 0), stop=(eo == 1 and j == 2))
                            return ins
                        S.op("pe", pv, reads=[vpad_r, pTs_r[j2]], writes=[psr[bo]])

                    def st_evac(idx):
                        blk, hp, eo = units[idx]
                        if eo != 1:
                            return
                        bo = 5 + (idx // 2) % 2
                        S.op("dve", (lambda e: e.tensor_copy(out=hT[:, hp, blk * 128:(blk + 1) * 128], in_=ps[bo][:, 0:128])),
                             reads=[psr[bo]], writes=[hT_r[hp]])

                    def ok(i):
                        return 0 <= i < NU
                    for t in range(NU + 5):
                        if ok(t - 2):
                            st_diag(t - 2)
                        if ok(t - 1):
                            st_max(t - 1)
                        if ok(t - 4):
                            st_evac(t - 4)
                        if ok(t - 3):
                            st_copy(t - 3)
                        if ok(t - 1):
                            st_exp(t - 1)
                        if ok(t):
                            st_qk(t)
                        if ok(t - 2):
                            st_tp(t - 2)
                        if ok(t - 3):
                            st_pv(t - 3)
                    if next_e0 is not None:
                        load_kv(next_e0)
                    linear(DC, 512, [wo_b[mb] for mb in range(4)], (lambda kc: hT[:, kc, 0:n]), hT_r, add_evac(n), n)
                    rmsnorm(3, n)
                    mlp(1, n)
                    for blk in range(4):
                        ob = oq % 2
                        oq += 1
                        for d0 in range(0, DC, 4):
                            bk = 4 + state["mb"] % 2
                            state["mb"] += 1

                            def tr(e, blk=blk, d0=d0, bk=bk):
                                ins = None
                                for i in range(4):
                                    ins = e.matmul(ps[bk][:, i * 128:(i + 1) * 128], lhsT=xT[:, d0 + i, blk * 128:(blk + 1) * 128],
                                                   rhs=idf_t[:], start=True, stop=True)
                                return ins
                            S.op("pe", tr, reads=xT_r[d0:d0 + 4], writes=[psr[bk]])
                            if (d0 // 4) % 2 == 0:
                                S.op("act", (lambda e, ob=ob, d0=d0, bk=bk: e.activation(out=xtok[ob][:, d0 * 128:(d0 + 4) * 128], in_=ps[bk][:, 0:512], func=ACT.Copy)),
                                     reads=[psr[bk]], writes=[xtok_r[ob]])
                            else:
                                S.op("dve", (lambda e, ob=ob, d0=d0, bk=bk: e.tensor_copy(out=xtok[ob][:, d0 * 128:(d0 + 4) * 128], in_=ps[bk][:, 0:512])),
                                     reads=[psr[bk]], writes=[xtok_r[ob]])
                        S.dma("sp", (lambda e, ob=ob, r0=y0 + blk * 128: e.dma_start(out=y_own[r0:r0 + 128, :], in_=xtok[ob][:])),
                              reads=[xtok_r[ob]], writes=[y_r])
                    return aq, oq
            for ti, (e0, y0, first, last) in enumerate(tiles2):
                nxt = tiles2[ti + 1][0] if ti + 1 < len(tiles2) else None
                aq, oq = t2_tile(e0, y0, first, last, aq, oq, ti == 0, nxt)
            S.barrier()
            st_t2.close()

            with nc.Block() as block:
                S.emit(block)
    return nc


def _bf(a):
    return np.ascontiguousarray(a.astype(ml_dtypes.bfloat16))


def _host_tables(j):
    t = {}
    for nm, N2 in (("wdft_p", N2_P), ("wdft_s", N2_S)):
        a = 2.0 * np.pi * np.outer(np.arange(N2), np.arange(N2)) / N2
        t[nm] = _bf(np.concatenate([np.cos(a), -np.sin(a)], axis=1))
    for nm, N2, K1, N, k1s in (("g_p", N2_P, K1_P, SEQ_P, 32 * j - 1), ("g_s", N2_S, K1_S, SEQ_S, 32 * j - 4)):
        k1 = (k1s + np.arange(K1)) % 128
        k = (N2 * k1[None, :] + np.arange(N2)[:, None]).astype(np.int64)
        n1 = np.arange(128, dtype=np.int64)
        ph = (n1[:, None, None] * k[None, :, :]) % N
        th = 2.0 * np.pi * ph.astype(np.float64) / N
        gr, gi = np.cos(th), -np.sin(th)
        t[nm] = _bf(np.concatenate([-gi, gr, gi], axis=2))
    c = (np.arange(2)[None, :] * 128 + np.arange(128)[:, None]).astype(np.int64)
    cp = np.arange(256, dtype=np.int64)
    th = 2.0 * np.pi * ((c[:, :, None] * cp[None, None, :]) % 256).astype(np.float64) / 256.0
    t["ctab"] = np.ascontiguousarray(np.stack([np.cos(th), np.sin(th)], axis=2).astype(np.float32))
    t["ident_f"] = np.eye(128, dtype=np.float32)
    t["ident_b"] = _bf(np.eye(128))
    t["ones_d"] = _bf(np.full((128, 128), 1.0 / D))
    oh = np.zeros((128, 128))
    oh[:64, :64] = 1.0 / 64
    oh[64:, 64:] = 1.0 / 64
    t["ones_h"] = _bf(oh)
    rt = np.zeros((128, 128))
    for hb in (0, 64):
        for d in range(8):
            rt[hb + d + 8, hb + d] = -1.0
            rt[hb + d, hb + d + 8] = 1.0
    t["rot_t"] = _bf(rt)
    hm = np.zeros((128, 2), np.float32)
    hm[:64, 0] = 1.0
    hm[64:, 1] = 1.0
    t["halfmask"] = hm
    t["attn_scale"] = np.full((128, 1), 0.125, np.float32)
    qi = np.arange(128)[:, None]
    kj = np.arange(384)[None, :]
    band = np.abs(qi + 128 - kj) <= 128
    m = np.zeros((128, 3, 384), np.float32)
    for mi in range(3):
        ok = band.copy()
        if mi == 0 and j == 0:
            ok &= kj >= 128
        if mi == 2 and j == 3:
            ok &= kj < 256
        m[:, mi, :] = np.where(ok, 0.0, NEGM)
    t["masks"] = _bf(m)
    pos = np.concatenate([(OWN_P * j - 128 + np.arange(EXT_P)) % SEQ_P, (OWN_S * j - 128 + np.arange(EXT_S)) % SEQ_S]).astype(np.float32)
    inv = (np.float32(500000.0) ** (-np.arange(0, 16, 2, dtype=np.float32) / np.float32(16))).astype(np.float32)
    ang = pos[None, :] * inv[:, None]
    cs = np.zeros((128, 2, EXT), np.float32)
    cs[:, 0, :] = 1.0
    for hb in (0, 64):
        for d in range(16):
            cs[hb + d, 0, :] = np.cos(ang[d % 8])
            cs[hb + d, 1, :] = np.sin(ang[d % 8])
    t["ropecs"] = cs
    return t


_NC_CACHE = {}


def kernel(x_prompt, x_sample, fourier_norm, fourier_w_out, attn_norm, attn_w_qkv, attn_q_norm,
           attn_k_norm, attn_sink, attn_w_o, mlp_norm, mlp_w_up, mlp_w_down):
    f32 = np.float32
    x_prompt = np.asarray(x_prompt, f32)
    x_sample = np.asarray(x_sample, f32)

    def fm(g):
        return np.asarray(g, f32).reshape(DC, 128).T
    gains = np.ascontiguousarray(np.stack([fm(fourier_norm[0]), fm(mlp_norm[0]), fm(attn_norm[0]), fm(mlp_norm[1])], axis=1))
    qk_gain = np.ascontiguousarray(np.stack([np.tile(np.asarray(attn_q_norm[0], f32), 2), np.tile(np.asarray(attn_k_norm[0], f32), 2)], axis=1))
    sinkb = np.ascontiguousarray(np.broadcast_to(np.asarray(attn_sink[0], f32)[None, :], (128, 32)))
    shared = dict(
        w_fout=np.ascontiguousarray(np.asarray(fourier_w_out[0], f32)),
        w_qkv=np.ascontiguousarray(np.asarray(attn_w_qkv[0], f32)),
        w_o=np.ascontiguousarray(np.asarray(attn_w_o[0], f32)),
        w_up=np.ascontiguousarray(np.asarray(mlp_w_up, f32)),
        w_down=np.ascontiguousarray(np.asarray(mlp_w_down, f32)),
        gains=gains, qk_gain=qk_gain, sinkb=sinkb,
    )
    in_maps = []
    for c in range(8):
        b, j = c // 4, c % 4
        ip = (OWN_P * j - 128 + np.arange(EXT_P)) % SEQ_P
        isx = (OWN_S * j - 128 + np.arange(EXT_S)) % SEQ_S
        m = dict(shared)
        m["xp_full"] = x_prompt[b]
        m["xs_full"] = x_sample[b]
        m["x_own"] = np.ascontiguousarray(np.concatenate([x_prompt[b][ip], x_sample[b][isx]], axis=0))
        m.update(_host_tables(j))
        in_maps.append(m)
    if "nc" not in _NC_CACHE:
        _NC_CACHE["nc"] = build_program()
    res = run_bass_kernel_spmd(_NC_CACHE["nc"], in_maps, core_ids=list(range(8)))
    y_p = np.empty((2, SEQ_P, D), f32)
    y_s = np.empty((2, SEQ_S, D), f32)
    for c in range(8):
        b, j = c // 4, c % 4
        y = np.asarray(res.results[c]["y_own"])
        y_p[b, OWN_P * j:OWN_P * (j + 1)] = y[:OWN_P]
        y_s[b, OWN_S * j:OWN_S * (j + 1)] = y[OWN_P:]
    return (y_p, y_s)
```

```python
import math
import contextlib
import numpy as np
import ml_dtypes
import concourse.bass as bass
import concourse.mybir as mybir
from concourse.bass_utils import run_bass_kernel_spmd

F32 = mybir.dt.float32
BF16 = mybir.dt.bfloat16
ACT = mybir.ActivationFunctionType
ALU = mybir.AluOpType
AX = mybir.AxisListType

D = 2048
DC = 16
DFF = 8192
SEQ_P, SEQ_S = 16384, 4096
OWN_P, OWN_S = 4096, 1024
K1_P, K1_S = 34, 40
N2_P, N2_S = 128, 32
EXT_P, EXT_S = K1_P * N2_P, K1_S * N2_S
EXT = EXT_P + EXT_S
OWN = OWN_P + OWN_S
EPS = 1e-6
NEGM = -30000.0
NDS = 16


class Res:
    __slots__ = ("name", "w", "rs", "const")

    def __init__(self, name, const=False):
        self.name, self.w, self.rs, self.const = name, None, [], const


class Sched:
    ENG = ["pe", "act", "dve", "pool", "sp"]

    def __init__(self, nc, stack):
        self.nc = nc
        self.ops = {e: [] for e in self.ENG}
        self.cnt = {e: 0 for e in self.ENG}
        self.esem = {e: stack.enter_context(nc.semaphore("es_" + e)) for e in ["pe", "act", "dve", "pool"]}
        self.dsem = {q: [stack.enter_context(nc.semaphore("ds_%s%d" % (q, i))) for i in range(NDS)] for q in ["sp", "pool"]}
        self.duse = {q: [0] * NDS for q in ["sp", "pool"]}
        self.dnext = {"sp": 0, "pool": 0}
        self.waited = {e: {} for e in self.ENG}

    def _collect(self, reads, writes):
        deps = set()
        for r in reads:
            if r.w is not None:
                deps.add(r.w)
        for r in writes:
            if r.w is not None:
                deps.add(r.w)
            deps.update(r.rs)
        return deps

    def _update(self, reads, writes, tok):
        for r in reads:
            if not r.const:
                r.rs.append(tok)
        for r in writes:
            r.w = tok
            r.rs = []

    def _waits(self, eng, deps):
        out = []
        for (key, sem, val, peng) in sorted(deps, key=lambda d: (d[0], d[2])):
            if peng == "pe" and eng == "pe":
                continue
            if self.waited[eng].get(key, 0) >= val:
                continue
            self.waited[eng][key] = val
            out.append((sem, val))
        return out

    def op(self, eng, fn, reads=(), writes=()):
        deps = self._collect(reads, writes)
        waits = self._waits(eng, deps)
        self.cnt[eng] += 1
        tok = ("e_" + eng, self.esem[eng], self.cnt[eng], eng)
        self.ops[eng].append((waits, fn, (self.esem[eng], 1)))
        self._update(reads, writes, tok)

    def dma(self, q, fn, reads=(), writes=()):
        deps = self._collect(reads, writes)
        i = self.dnext[q]
        self.dnext[q] = (i + 1) % NDS
        sem = self.dsem[q][i]
        key = "d_%s%d" % (q, i)
        prev = self.duse[q][i]
        if prev > 0:
            deps.add((key, sem, 16 * prev, "dma"))
        self.duse[q][i] = prev + 1
        waits = self._waits(q, deps)
        tok = (key, sem, 16 * (prev + 1), "dma")
        self.ops[q].append((waits, fn, (sem, 16)))
        self._update(reads, writes, tok)

    def barrier(self):
        toks = []
        for e in ["pe", "act", "dve", "pool"]:
            if self.cnt[e] > 0:
                toks.append(("e_" + e, self.esem[e], self.cnt[e], "bar"))
        for q in ["sp", "pool"]:
            for i in range(NDS):
                if self.duse[q][i] > 0:
                    toks.append(("d_%s%d" % (q, i), self.dsem[q][i], 16 * self.duse[q][i], "bar"))
        for e in self.ENG:
            w = self._waits(e, toks)
            if w:
                self.ops[e].append((w, None, None))

    def emit(self, block):
        def run(e):
            def body(eng):
                for (waits, fn, inc) in self.ops[e]:
                    for (sem, val) in waits:
                        eng.wait_ge(sem, val)
                    if fn is not None:
                        ins = fn(eng)
                        ins.then_inc(inc[0], inc[1])
            return body
        block.tensor(run("pe"))
        block.scalar(run("act"))
        block.vector(run("dve"))
        block.gpsimd(run("pool"))
        block.sync(run("sp"))


def build_program():
    nc = bass.Bass("TRN2", target_bir_lowering=False)

    def din(name, shape, dt=F32):
        return nc.dram_tensor(name, list(shape), dt, kind="ExternalInput").ap()

    xp_full = din("xp_full", [SEQ_P, D])
    xs_full = din("xs_full", [SEQ_S, D])
    x_own = din("x_own", [EXT, D])
    w_fout = din("w_fout", [D, D])
    w_qkv = din("w_qkv", [D, 2560])
    w_o = din("w_o", [D, D])
    w_up = din("w_up", [2, D, DFF])
    w_down = din("w_down", [2, DFF, D])
    gains = din("gains", [128, 4, DC])
    qk_gain = din("qk_gain", [128, 2])
    sinkb = din("sinkb", [128, 32])
    attn_scale = din("attn_scale", [128, 1])
    halfmask = din("halfmask", [128, 2])
    masks = din("masks", [128, 3, 384], BF16)
    ropecs = din("ropecs", [128, 2, EXT])
    ctab = din("ctab", [128, 2, 2, 256])
    wdft_p = din("wdft_p", [N2_P, 2 * N2_P], BF16)
    wdft_s = din("wdft_s", [N2_S, 2 * N2_S], BF16)
    g_p = din("g_p", [128, N2_P, 3 * K1_P], BF16)
    g_s = din("g_s", [128, N2_S, 3 * K1_S], BF16)
    ident_f = din("ident_f", [128, 128])
    ident_b = din("ident_b", [128, 128], BF16)
    ones_d = din("ones_d", [128, 128], BF16)
    ones_h = din("ones_h", [128, 128], BF16)
    rot_t = din("rot_t", [128, 128], BF16)
    y_own = nc.dram_tensor("y_own", [OWN, D], F32, kind="ExternalOutput").ap()

    fT_s = nc.dram_tensor("fT_s", [D, EXT], BF16, kind="Internal").ap()
    x2T_s = nc.dram_tensor("x2T_s", [D, EXT], F32, kind="Internal").ap()
    qT_s = nc.dram_tensor("qT_s", [D, EXT], BF16, kind="Internal").ap()
    kT_s = nc.dram_tensor("kT_s", [4, 2, 128, EXT], BF16, kind="Internal").ap()
    v_s = nc.dram_tensor("v_s", [EXT, 256], BF16, kind="Internal").ap()
    wfout_b = nc.dram_tensor("wfout_b", [4, 128, 16, 512], BF16, kind="Internal").ap()
    wq_b = nc.dram_tensor("wq_b", [4, 128, 16, 512], BF16, kind="Internal").ap()
    wk_b = nc.dram_tensor("wk_b", [1, 128, 16, 512], BF16, kind="Internal").ap()
    wo_b = nc.dram_tensor("wo_b", [4, 128, 16, 512], BF16, kind="Internal").ap()
    wup_b = nc.dram_tensor("wup_b", [2, 16, 128, 16, 512], BF16, kind="Internal").ap()
    wdown_b = nc.dram_tensor("wdown_b", [2, 2, 8, 128, 32, 256], BF16, kind="Internal").ap()

    with contextlib.ExitStack() as stack:
        S = Sched(nc, stack)

        def sb(name, shape, dt):
            return stack.enter_context(nc.sbuf_tensor(name, list(shape), dt))

        ps = [stack.enter_context(nc.psum_tensor("ps%d" % i, [128, 512], F32)) for i in range(8)]
        psr = [Res("ps%d" % i) for i in range(8)]

        gains_t = sb("gains_t", [128, 4, DC], F32)
        qkg_t = sb("qkg_t", [128, 2], F32)
        sink_t = sb("sink_t", [128, 32], F32)
        asc_t = sb("asc_t", [128, 1], F32)
        hm_t = sb("hm_t", [128, 2], F32)
        idf_t = sb("idf_t", [128, 128], F32)
        idb_t = sb("idb_t", [128, 128], BF16)
        onesd_t = sb("onesd_t", [128, 128], BF16)
        onesh_t = sb("onesh_t", [128, 128], BF16)
        rot_tt = sb("rot_tt", [128, 128], BF16)
        rstd_seq = sb("rstd_seq", [128, 2, 128], F32)
        CONST = Res("const")

        def ld(dst, src):
            S.dma("sp", lambda e: e.dma_start(out=dst, in_=src), writes=[CONST])
        ld(gains_t[:], gains)
        ld(qkg_t[:], qk_gain)
        ld(sink_t[:], sinkb)
        ld(asc_t[:], attn_scale)
        ld(hm_t[:], halfmask)
        ld(idf_t[:], ident_f)
        ld(idb_t[:], ident_b)
        ld(onesd_t[:], ones_d)
        ld(onesh_t[:], ones_h)
        ld(rot_tt[:], rot_t)
        S.barrier()
        S.op("dve", lambda e: e.tensor_tensor(out=qkg_t[:, 0:1], in0=qkg_t[:, 0:1], in1=asc_t[:], op=ALU.mult),
             reads=[CONST], writes=[CONST])
        S.barrier()
        CONST.const = True
        CONST.w = None
        CONST.rs = []

        conv = []

        def cv(dst, src):
            conv.append(lambda: S.dma("pool", (lambda e: e.dma_start(out=dst, in_=src)), writes=[CONVW]))
        CONVW = Res("convw")
        for mb in range(4):
            cv(wfout_b[mb], w_fout[:, mb * 512:(mb + 1) * 512].rearrange("(kc p) m -> p kc m", p=128))

        def cv_mlp(li):
            for mb in range(16):
                cv(wup_b[li, mb], w_up[li][:, mb * 512:(mb + 1) * 512].rearrange("(kc p) m -> p kc m", p=128))
            for half in range(2):
                for mb in range(8):
                    cv(wdown_b[li, half, mb], w_down[li][half * 4096:(half + 1) * 4096, mb * 256:(mb + 1) * 256].rearrange("(kc p) m -> p kc m", p=128))
        cv_mlp(0)
        for mb in range(4):
            cv(wq_b[mb], w_qkv[:, mb * 512:(mb + 1) * 512].rearrange("(kc p) m -> p kc m", p=128))
        for kv in range(4):
            for dup in range(2):
                cv(wk_b[0][:, :, kv * 128 + dup * 64:kv * 128 + dup * 64 + 64],
                   w_qkv[:, 2048 + kv * 64:2048 + (kv + 1) * 64].rearrange("(kc p) d -> p kc d", p=128))
        n_conv0 = len(conv)
        for mb in range(4):
            cv(wo_b[mb], w_o[:, mb * 512:(mb + 1) * 512].rearrange("(kc p) m -> p kc m", p=128))
        cv_mlp(1)
        n_conv1 = len(conv) - n_conv0

        seqs = [
            dict(x=xp_full, N2=N2_P, K1=K1_P, G=g_p, W=wdft_p, eoff=0, ext=EXT_P, scale=1.0 / math.sqrt(SEQ_P * 256.0), si=0),
            dict(x=xs_full, N2=N2_S, K1=K1_S, G=g_s, W=wdft_s, eoff=EXT_P, ext=EXT_S, scale=1.0 / math.sqrt(SEQ_S * 256.0), si=1),
        ]

        with contextlib.ExitStack() as st0:
            def sb0(name, shape, dt):
                return st0.enter_context(nc.sbuf_tensor(name, list(shape), dt))
            xin = [sb0("xin%d" % i, [128, 2, D], F32) for i in range(3)]
            xin_r = [Res("xin%d" % i) for i in range(3)]
            junk = sb0("junk", [128, D], BF16)
            junk2 = sb0("junk2", [128, D], BF16)
            ss = sb0("ss", [128, 2, 128], F32)
            lnt = sb0("lnt", [128, 128], F32)
            ss_r = Res("ss")
            lnt_r = Res("lnt")
            rstd_r = Res("rstd")
            rot_r = {"act": [Res("f0a%d" % i) for i in range(4)], "dve": [Res("f0d%d" % i) for i in range(4)]}
            S.op("dve", lambda e: e.memset(ss[:], 0.0), writes=[ss_r])
            S.barrier()
            it = 0
            k_ = {"act": 0, "dve": 0}
            for sq in seqs:
                N2 = sq["N2"]
                xv = sq["x"].rearrange("(a n) c -> a n c", n=128)
                for n1 in range(0, 128, 2):
                    b = it % 3
                    it += 1
                    S.dma("sp", (lambda e, b=b, n1=n1, xv=xv, N2=N2: e.dma_start(out=xin[b][0:N2, :, :], in_=xv[:, n1:n1 + 2, :])),
                          writes=[xin_r[b]])
                    for t in range(2):
                        eng = "act" if t == 0 else "dve"
                        rr = rot_r[eng][k_[eng] % 4]
                        k_[eng] += 1
                        if eng == "act":
                            S.op("act", (lambda e, b=b, t=t, n1=n1, N2=N2, si=sq["si"]: e.activation(
                                out=junk[0:N2, :], in_=xin[b][0:N2, t, :], func=ACT.Square,
                                accum_out=ss[0:N2, si, n1 + t:n1 + t + 1])),
                                reads=[xin_r[b]], writes=[rr])
                        else:
                            S.op("dve", (lambda e, b=b, t=t, n1=n1, N2=N2, si=sq["si"]: e.scalar_tensor_tensor(
                                out=junk2[0:N2, :], in0=xin[b][0:N2, t, :], scalar=1.0, in1=xin[b][0:N2, t, :],
                                op0=ALU.mult, op1=ALU.mult, accum_out=ss[0:N2, si, n1 + t:n1 + t + 1])),
                                reads=[xin_r[b]], writes=[rr])
            S.barrier()
            for sq in seqs:
                si = sq["si"]
                S.op("act", (lambda e, si=si: e.activation(out=lnt[:], in_=ss[:, si, :], func=ACT.Ln, bias=EPS, scale=1.0 / D)),
                     reads=[ss_r], writes=[lnt_r])
                S.op("act", (lambda e, si=si: e.activation(out=rstd_seq[:, si, :], in_=lnt[:], func=ACT.Exp, scale=-0.5)),
                     reads=[lnt_r], writes=[rstd_r])
            S.barrier()

        with contextlib.ExitStack() as st1:
            def sb1(name, shape, dt):
                return st1.enter_context(nc.sbuf_tensor(name, list(shape), dt))
            Yb = sb1("Yb", [128, 128, 128], BF16)
            U = sb1("U", [128, 128 * 2 * 128], BF16)
            ZT = sb1("ZT", [128, 2, 2, EXT_P], BF16)
            Gt = sb1("Gt", [128, N2_P * 3 * K1_P], BF16)
            Wt = sb1("Wt", [128, 256], BF16)
            CG = sb1("CG", [128, DC, 2, 256], BF16)
            ctab_t = sb1("ctab_t", [128, 2, 2, 256], F32)
            fst = [sb1("fst%d" % i, [128, 512], BF16) for i in range(4)]
            ystg = [sb1("ystg%d" % i, [128, 16, 128], F32) for i in range(3)]
            ystg_r = [Res("ystg%d" % i) for i in range(3)]
            stq = [0]
            Yb_r = [Res("Yb%d" % i) for i in range(8)]
            U_r, Gt_r, Wt_r, CG_r = Res("U"), Res("Gt"), Res("Wt"), Res("CG")
            ZT_r = [Res("ZT0"), Res("ZT1")]
            fst_r = [Res("fst%d" % i) for i in range(4)]
            ctab_r = Res("ctab")
            fT_r = Res("fT_s")

            S.dma("sp", lambda e: e.dma_start(out=ctab_t[:], in_=ctab), writes=[ctab_r])
            for ch in range(DC):
                kc = ch % 2
                for cs in range(2):
                    S.op("dve", (lambda e, ch=ch, kc=kc, cs=cs: e.tensor_scalar(
                        out=CG[:, ch, cs, :], in0=ctab_t[:, kc, cs, :], scalar1=gains_t[:, 0, ch:ch + 1], scalar2=None, op0=ALU.mult)),
                        reads=[ctab_r], writes=[CG_r])

            cnt1 = dict(evq=0, fq=0)

            def f1_views(sq):
                N2, K1 = sq["N2"], sq["K1"]
                xv = sq["x"].rearrange("(a n) c -> a n c", n=128)
                Gv = Gt[:, 0:N2 * 3 * K1].rearrange("p (k c) -> p k c", c=3 * K1)
                Uv = U[:, 0:N2 * 256].rearrange("p (c r k) -> p c r k", r=2, k=N2)
                return xv, Gv, Uv

            def f1_prep(sq, ch):
                N2, si = sq["N2"], sq["si"]
                xv, Gv, Uv = f1_views(sq)
                for _ in range(2):
                    if len(conv) > n_conv1:
                        conv.pop(0)()
                for pc in range(8):
                    sg = stq[0] % 3
                    stq[0] += 1
                    S.dma("sp", (lambda e, pc=pc, sg=sg: e.dma_start(
                        out=ystg[sg][0:N2, :, :], in_=xv[:, pc * 16:(pc + 1) * 16, ch * 128:(ch + 1) * 128])),
                        writes=[ystg_r[sg]])
                    eng = "pool" if (si == 0 or pc % 2 == 0) else "dve"
                    S.op(eng, (lambda e, pc=pc, sg=sg: e.tensor_tensor(
                        out=Yb[0:N2, pc * 16:(pc + 1) * 16, :], in0=ystg[sg][0:N2, :, :],
                        in1=rstd_seq[0:N2, si, pc * 16:(pc + 1) * 16].unsqueeze(2).to_broadcast([N2, 16, 128]), op=ALU.mult)),
                        reads=[ystg_r[sg]], writes=[Yb_r[pc]])

            def f1_s1(sq, ch):
                N2 = sq["N2"]
                xv, Gv, Uv = f1_views(sq)
                nb1 = 512 // (2 * N2)
                if ch == 0:
                    S.dma("sp", (lambda e: e.dma_start(out=Wt[0:N2, 0:2 * N2], in_=sq["W"])), writes=[Wt_r])
                for c0 in range(0, 128, nb1):
                    bk = (c0 // nb1) % 4

                    def s1(e, c0=c0, bk=bk):
                        ins = None
                        for i in range(nb1):
                            ins = e.matmul(ps[bk][:, i * 2 * N2:(i + 1) * 2 * N2], lhsT=Yb[0:N2, :, c0 + i],
                                           rhs=Wt[0:N2, 0:2 * N2], start=True, stop=True)
                        return ins
                    S.op("pe", s1, reads=Yb_r + [Wt_r], writes=[psr[bk]])
                    src = ps[bk][:, 0:nb1 * 2 * N2].rearrange("p (c r k) -> p c r k", r=2, k=N2)
                    dst = Uv[:, c0:c0 + nb1, :, :]
                    eng = "act" if cnt1["evq"] % 2 == 0 else "dve"
                    cnt1["evq"] += 1
                    if eng == "act":
                        S.op("act", (lambda e, src=src, dst=dst: e.activation(out=dst, in_=src, func=ACT.Copy)),
                             reads=[psr[bk]], writes=[U_r])
                    else:
                        S.op("dve", (lambda e, src=src, dst=dst: e.tensor_copy(out=dst, in_=src)),
                             reads=[psr[bk]], writes=[U_r])

            def f1_s2(sq, ch):
                N2, K1, ext = sq["N2"], sq["K1"], sq["ext"]
                xv, Gv, Uv = f1_views(sq)
                ci = ch % 2
                nk = 512 // (2 * K1)
                if ch == 0:
                    S.dma("sp", (lambda e: e.dma_start(out=Gv, in_=sq["G"])), writes=[Gt_r])
                ZTv = ZT[:, ci, :, 0:ext].rearrange("p r (a k) -> p r a k", k=N2)
                for k0 in range(0, N2, nk):
                    n = min(nk, N2 - k0)
                    bk = 4 + (k0 // nk) % 4

                    def s2(e, k0=k0, n=n, bk=bk):
                        ins = None
                        for i in range(n):
                            o = ps[bk][:, i * 2 * K1:(i + 1) * 2 * K1]
                            e.matmul(o, lhsT=Uv[:, :, 0, k0 + i], rhs=Gv[:, k0 + i, K1:3 * K1], start=True, stop=False)
                            ins = e.matmul(o, lhsT=Uv[:, :, 1, k0 + i], rhs=Gv[:, k0 + i, 0:2 * K1], start=False, stop=True)
                        return ins
                    S.op("pe", s2, reads=[U_r, Gt_r], writes=[psr[bk]])
                    src = ps[bk][:, 0:n * 2 * K1].rearrange("p (k r a) -> p k r a", r=2, a=K1)
                    dst = ZTv[:, :, :, k0:k0 + n].rearrange("p r a k -> p k r a")
                    eng = "act" if cnt1["evq"] % 2 == 0 else "dve"
                    cnt1["evq"] += 1
                    if eng == "act":
                        S.op("act", (lambda e, src=src, dst=dst: e.activation(out=dst, in_=src, func=ACT.Copy)),
                             reads=[psr[bk]], writes=[ZT_r[ci]])
                    else:
                        S.op("dve", (lambda e, src=src, dst=dst: e.tensor_copy(out=dst, in_=src)),
                             reads=[psr[bk]], writes=[ZT_r[ci]])
                if ci == 1:
                    g = ch // 2
                    for t0 in range(0, ext, 512):
                        n = min(512, ext - t0)
                        for mc in range(2):
                            bk = (cnt1["fq"] % 4)

                            def cd(e, mc=mc, t0=t0, n=n, bk=bk):
                                ins = None
                                idx = 0
                                for kc in range(2):
                                    for cs in range(2):
                                        ins = e.matmul(ps[bk][:, 0:n], lhsT=CG[:, g * 2 + kc, cs, mc * 128:(mc + 1) * 128],
                                                       rhs=ZT[:, kc, cs, t0:t0 + n], start=(idx == 0), stop=(idx == 3))
                                        idx += 1
                                return ins
                            S.op("pe", cd, reads=[CG_r] + ZT_r, writes=[psr[bk]])
                            fb = cnt1["fq"] % 4
                            cnt1["fq"] += 1
                            S.op("act", (lambda e, fb=fb, bk=bk, n=n: e.activation(
                                out=fst[fb][:, 0:n], in_=ps[bk][:, 0:n], func=ACT.Copy, scale=sq["scale"])),
                                reads=[psr[bk]], writes=[fst_r[fb]])
                            row = (g * 2 + mc) * 128
                            S.dma("sp", (lambda e, fb=fb, n=n, row=row, col=sq["eoff"] + t0: e.dma_start(
                                out=fT_s[row:row + 128, col:col + n], in_=fst[fb][:, 0:n])),
                                reads=[fst_r[fb]], writes=[fT_r])

            items = [(sq, ch) for sq in seqs for ch in range(DC)]
            f1_prep(*items[0])
            for ii, it in enumerate(items):
                f1_s1(*it)
                if ii + 1 < len(items):
                    f1_prep(*items[ii + 1])
                f1_s2(*it)
            while len(conv) > n_conv1:
                conv.pop(0)()
            S.barrier()

        with contextlib.ExitStack() as st2:
            def sb2(name, shape, dt):
                return st2.enter_context(nc.sbuf_tensor(name, list(shape), dt))
            xT = sb2("xT", [128, DC, 512], F32)
            hT = sb2("hT", [128, DC, 512], BF16)
            aT = sb2("aT", [128, 32, 512], BF16)
            NW = 3
            wbuf = [sb2("wbuf%d" % i, [128, 8192], BF16) for i in range(NW)]
            NT = 4
            tmpf = [sb2("tmpf%d" % i, [128, 512], F32) for i in range(NT)]
            tmpg = [sb2("tmpg%d" % i, [128, 512], F32) for i in range(NT)]
            tmpb = [sb2("tmpb%d" % i, [128, 512], BF16) for i in range(NT)]
            rstd_t = sb2("rstd_t", [128, 512], F32)
            rstdT = sb2("rstdT", [128, 4], F32)
            rstdT_r = Res("rstdT")
            xtok = [sb2("xtok%d" % i, [128, D], F32) for i in range(2)]
            kst = sb2("kst", [128, 4, 2, 768], BF16)
            vst = sb2("vst", [128, 6, 256], BF16)

            xT_r = [Res("xT%d" % i) for i in range(DC)]
            hT_r = [Res("hT%d" % i) for i in range(DC)]
            aT_r = [Res("aT%d" % i) for i in range(32)]
            wbuf_r = [Res("wbuf%d" % i) for i in range(NW)]
            wv_r = Res("wv")
            tmpf_r = [Res("tmpf%d" % i) for i in range(NT)]
            tmpg_r = [Res("tmpg%d" % i) for i in range(NT)]
            tmpb_r = [Res("tmpb%d" % i) for i in range(NT)]
            rstd_r2, lnt2_r, cs_r = Res("rstd_t"), Res("lnt2"), Res("cs_t")
            xtok_r = [Res("xtok0"), Res("xtok1")]
            kst_r, vst_r, vpad_r = Res("kst"), Res("vst"), Res("vpad")
            pbf_r = [Res("pbf%d" % i) for i in range(3)]
            dgb_r = [Res("dgb%d" % i) for i in range(3)]
            negm_r = [Res("negm%d" % i) for i in range(3)]
            rsum_r = [Res("rsum%d" % i) for i in range(3)]
            rden_r = [Res("rden%d" % i) for i in range(3)]
            pTs_r = [Res("pTs0"), Res("pTs1")]
            stat_r = [Res("stat0"), Res("stat1")]
            mask_r = Res("mask")
            fT_r2, x2T_r, qT_r, kT_r, v_r, y_r = Res("fT2"), Res("x2T"), Res("qTs"), Res("kTs"), Res("vs"), Res("y")

            state = dict(w=0, lb=0, ev=0, mb=0)

            def linear(KC, MBW, blocks, rhs, rhs_res, evac, n, mc0=0, tail=None):
                nmi = MBW // 128
                pend = []
                for mb, blk in enumerate(blocks):
                    wi = state["w"] % NW
                    state["w"] += 1
                    wv = wbuf[wi][:, 0:KC * MBW].rearrange("p (k m) -> p k m", m=MBW)
                    S.dma("sp", (lambda e, wv=wv, blk=blk: e.dma_start(out=wv, in_=blk)), writes=[wbuf_r[wi]])
                    for mi in range(nmi):
                        bk = state["lb"] % 4
                        state["lb"] += 1

                        def mm(e, wv=wv, bk=bk, mi=mi):
                            ins = None
                            for kc in range(KC):
                                ins = e.matmul(ps[bk][:, 0:n], lhsT=wv[:, kc, mi * 128:(mi + 1) * 128], rhs=rhs(kc), start=(kc == 0), stop=(kc == KC - 1))
                            return ins
                        S.op("pe", mm, reads=[wbuf_r[wi]] + list(rhs_res), writes=[psr[bk]])
                        st = evac(mc0 + mb * nmi + mi, bk)
                        if st:
                            pend.append(list(st))
                        for lag, stages in enumerate(reversed(pend[:-1] if st else pend)):
                            pass
                        for stages in pend[:-1] if st else pend:
                            if stages:
                                stages.pop(0)()
                        while pend and not pend[0]:
                            pend.pop(0)
                if tail is not None:
                    tail(pend)
                while pend:
                    for stages in pend:
                        if stages:
                            stages.pop(0)()
                    while pend and not pend[0]:
                        pend.pop(0)

            def rmsnorm(gi, n, need_T=False):
                for dc in range(DC):
                    S.op("dve", (lambda e, dc=dc: e.tensor_scalar(out=hT[:, dc, 0:n], in0=xT[:, dc, 0:n], scalar1=gains_t[:, gi, dc:dc + 1],
                                                                  scalar2=None, op0=ALU.mult)),
                         reads=[xT_r[dc]], writes=[hT_r[dc]])
                for dc in range(DC):
                    if dc % 2 == 0:
                        S.op("act", (lambda e, dc=dc: e.activation(out=aT[:, dc, 0:n], in_=xT[:, dc, 0:n], func=ACT.Square)),
                             reads=[xT_r[dc]], writes=[aT_r[dc]])
                    else:
                        S.op("pool", (lambda e, dc=dc: e.tensor_tensor(out=aT[:, dc, 0:n], in0=xT[:, dc, 0:n], in1=xT[:, dc, 0:n], op=ALU.mult)),
                             reads=[xT_r[dc]], writes=[aT_r[dc]])
                bk = 4 + state["mb"] % 4
                state["mb"] += 1

                def mm(e, bk=bk):
                    ins = None
                    for dc in range(DC):
                        ins = e.matmul(ps[bk][:, 0:n], lhsT=onesd_t[:], rhs=aT[:, dc, 0:n], start=(dc == 0), stop=(dc == DC - 1))
                    return ins
                S.op("pe", mm, reads=aT_r[0:DC], writes=[psr[bk]])
                S.op("act", (lambda e, bk=bk: e.activation(out=rstd_t[:, 0:n], in_=ps[bk][:, 0:n], func=ACT.Ln, bias=EPS, scale=1.0)),
                     reads=[psr[bk]], writes=[rstd_r2])
                S.op("act", (lambda e: e.activation(out=rstd_t[:, 0:n], in_=rstd_t[:, 0:n], func=ACT.Exp, scale=-0.5)),
                     reads=[rstd_r2], writes=[rstd_r2])
                if need_T:
                    nblk = n // 128
                    b2 = 4 + state["mb"] % 4
                    state["mb"] += 1

                    def tt(e, b2=b2):
                        ins = None
                        for blk in range(nblk):
                            ins = e.matmul(ps[b2][:, blk:blk + 1], lhsT=rstd_t[0:1, blk * 128:(blk + 1) * 128], rhs=idf_t[0:1, 0:1],
                                           start=True, stop=True)
                        return ins
                    S.op("pe", tt, reads=[rstd_r2], writes=[psr[b2]])
                    S.op("act", (lambda e, b2=b2: e.activation(out=rstdT[:, 0:nblk], in_=ps[b2][:, 0:nblk], func=ACT.Copy)),
                         reads=[psr[b2]], writes=[rstdT_r])

            def add_evac(n):
                def f(mc, bk):
                    S.op("dve", (lambda e: e.tensor_tensor(out=xT[:, mc, 0:n], in0=ps[bk][:, 0:n], in1=xT[:, mc, 0:n], op=ALU.add)),
                         reads=[psr[bk], xT_r[mc]], writes=[xT_r[mc]])
                return f

            def mlp(li, n):
                for half in range(2):
                    def up_evac(mc, bk):
                        i = state["ev"] % NT
                        state["ev"] += 1
                        S.op("act", (lambda e: e.activation(out=tmpf[i][:, 0:n], in_=ps[bk][:, 0:n], func=ACT.Relu)),
                             reads=[psr[bk]], writes=[tmpf_r[i]])
                        S.op("dve", (lambda e: e.tensor_tensor(out=tmpf[i][:, 0:n], in0=tmpf[i][:, 0:n], in1=rstd_t[:, 0:n], op=ALU.mult)),
                             reads=[tmpf_r[i], rstd_r2], writes=[tmpf_r[i]])
                        S.op("pool", (lambda e: e.tensor_tensor(out=aT[:, mc, 0:n], in0=tmpf[i][:, 0:n], in1=tmpf[i][:, 0:n], op=ALU.mult)),
                             reads=[tmpf_r[i]], writes=[aT_r[mc]])
                    linear(DC, 512, [wup_b[li, half * 8 + mb] for mb in range(8)], (lambda kc: hT[:, kc, 0:n]), hT_r, up_evac, n)
                    linear(32, 256, [wdown_b[li, half, mb] for mb in range(8)], (lambda kc: aT[:, kc, 0:n]), aT_r, add_evac(n), n)

            st_t1 = contextlib.ExitStack()
            wv_t = st_t1.enter_context(nc.sbuf_tensor("wv_t", [128, DC, 256], BF16))
            cs_t = st_t1.enter_context(nc.sbuf_tensor("cs_t", [128, 2, 512], F32))
            S.dma("pool", lambda e: e.dma_start(out=wv_t[:], in_=w_qkv[:, 2304:2560].rearrange("(kc p) m -> p kc m", p=128)), writes=[wv_r])

            tiles1 = [(t * 512, 512) for t in range(EXT // 512)]
            xq = 0
            def x_load(r0, xb_):
                S.dma("sp", (lambda e: e.dma_start(out=xtok[xb_][:], in_=x_own[r0:r0 + 128, :])), writes=[xtok_r[xb_]])

            def t1_tile(e0, n, xq, prefetched, next_e0):
                    nblk = n // 128
                    for blk in range(nblk):
                        xb_ = xq % 2
                        xq += 1
                        if not (prefetched and blk < 2):
                            x_load(e0 + blk * 128, xb_)
                        for d0 in range(0, DC, 4):
                            bk = 4 + state["mb"] % 4
                            state["mb"] += 1

                            def tr(e, xb_=xb_, d0=d0, bk=bk):
                                ins = None
                                for i in range(4):
                                    ins = e.matmul(ps[bk][:, i * 128:(i + 1) * 128], lhsT=xtok[xb_][:, (d0 + i) * 128:(d0 + i + 1) * 128],
                                                   rhs=idf_t[:], start=True, stop=True)
                                return ins
                            S.op("pe", tr, reads=[xtok_r[xb_]], writes=[psr[bk]])
                            eng = "act" if (d0 // 4) % 2 == 0 else "dve"
                            src = ps[bk][:, 0:512].rearrange("p (c t) -> p c t", t=128)
                            dst = xT[:, d0:d0 + 4, blk * 128:(blk + 1) * 128]
                            if eng == "act":
                                S.op("act", (lambda e, src=src, dst=dst: e.activation(out=dst, in_=src, func=ACT.Copy)),
                                     reads=[psr[bk]], writes=xT_r[d0:d0 + 4])
                            else:
                                S.op("dve", (lambda e, src=src, dst=dst: e.tensor_copy(out=dst, in_=src)),
                                     reads=[psr[bk]], writes=xT_r[d0:d0 + 4])
                    S.dma("sp", (lambda e, e0=e0, n=n: e.dma_start(out=hT[:, :, 0:n], in_=fT_s[:, e0:e0 + n].rearrange("(kc p) t -> p kc t", p=128))),
                          reads=[fT_r2], writes=hT_r)
                    linear(DC, 512, [wfout_b[mb] for mb in range(4)], (lambda kc: hT[:, kc, 0:n]), hT_r, add_evac(n), n)
                    rmsnorm(1, n)
                    mlp(0, n)
                    if next_e0 is not None:
                        x_load(next_e0, xq % 2)
                        x_load(next_e0 + 128, (xq + 1) % 2)
                    S.dma("sp", (lambda e, e0=e0, n=n: e.dma_start(out=x2T_s[:, e0:e0 + n].rearrange("(kc p) t -> p kc t", p=128), in_=xT[:, :, 0:n])),
                          reads=xT_r, writes=[x2T_r])
                    rmsnorm(2, n, need_T=True)
                    S.dma("sp", (lambda e, e0=e0, n=n: e.dma_start(out=cs_t[:, :, 0:n], in_=ropecs[:, :, e0:e0 + n])), writes=[cs_r])

                    def qk_wsrc(mc):
                        if mc < 16:
                            return wsrc_std(w_qkv, DC)(mc)
                        kv = mc - 16
                        src = w_qkv[:, 2048 + kv * 64:2048 + (kv + 1) * 64].rearrange("(kc p) m -> p kc m", p=128)
                        return [((lambda wt: wt[:, 0:16, 0:64]), src), ((lambda wt: wt[:, 0:16, 64:128]), src)]

                    def qk_evac(mc, bk):
                        i = state["ev"] % NT
                        state["ev"] += 1
                        gcol = 0 if mc < 16 else 1
                        S.op("dve", (lambda e: e.tensor_tensor(out=tmpf[i][:, 0:n], in0=ps[bk][:, 0:n], in1=rstd_t[:, 0:n], op=ALU.mult)),
                             reads=[psr[bk], rstd_r2], writes=[tmpf_r[i]])
                        S.op("pool", (lambda e: e.tensor_tensor(out=tmpb[i][:, 0:n], in0=tmpf[i][:, 0:n], in1=tmpf[i][:, 0:n], op=ALU.mult)),
                             reads=[tmpf_r[i]], writes=[tmpb_r[i]])
                        loc = {}

                        def stage2():
                            b2 = 4 + state["mb"] % 4
                            state["mb"] += 1
                            S.op("pe", (lambda e: e.matmul(ps[b2][:, 0:n], lhsT=onesh_t[:], rhs=tmpb[i][:, 0:n], start=True, stop=True)),
                                 reads=[tmpb_r[i]], writes=[psr[b2]])
                            S.op("act", (lambda e: e.activation(out=tmpg[i][:, 0:n], in_=ps[b2][:, 0:n], func=ACT.Ln, bias=EPS, scale=1.0)),
                                 reads=[psr[b2]], writes=[tmpg_r[i]])
                            S.op("act", (lambda e: e.activation(out=tmpg[i][:, 0:n], in_=tmpg[i][:, 0:n], func=ACT.Exp, scale=-0.5)),
                                 reads=[tmpg_r[i]], writes=[tmpg_r[i]])
                            S.op("dve", (lambda e: e.scalar_tensor_tensor(out=tmpb[i][:, 0:n], in0=tmpf[i][:, 0:n], scalar=qkg_t[:, gcol:gcol + 1],
                                                                          in1=tmpg[i][:, 0:n], op0=ALU.mult, op1=ALU.mult)),
                                 reads=[tmpf_r[i], tmpg_r[i], tmpb_r[i]], writes=[tmpb_r[i]])

                        def stage3():
                            b3 = 4 + state["mb"] % 4
                            state["mb"] += 1
                            loc["b3"] = b3
                            S.op("pe", (lambda e: e.matmul(ps[b3][:, 0:n], lhsT=rot_tt[:], rhs=tmpb[i][:, 0:n], start=True, stop=True)),
                                 reads=[tmpb_r[i]], writes=[psr[b3]])
                            S.op("pool", (lambda e: e.tensor_tensor(out=tmpf[i][:, 0:n], in0=tmpb[i][:, 0:n], in1=cs_t[:, 0, 0:n], op=ALU.mult)),
                                 reads=[tmpb_r[i], cs_r], writes=[tmpf_r[i]])

                        def stage4():
                            b3 = loc["b3"]
                            S.op("dve", (lambda e: e.tensor_tensor(out=tmpg[i][:, 0:n], in0=ps[b3][:, 0:n], in1=cs_t[:, 1, 0:n], op=ALU.mult)),
                                 reads=[psr[b3], cs_r], writes=[tmpg_r[i]])
                            if mc < 16:
                                S.op("pool", (lambda e: e.tensor_tensor(out=aT[:, mc, 0:n], in0=tmpf[i][:, 0:n], in1=tmpg[i][:, 0:n], op=ALU.add)),
                                     reads=[tmpf_r[i], tmpg_r[i]], writes=[aT_r[mc]])
                            else:
                                kv = mc - 16
                                S.op("pool", (lambda e: e.tensor_tensor(out=tmpf[i][:, 0:n], in0=tmpf[i][:, 0:n], in1=tmpg[i][:, 0:n], op=ALU.add)),
                                     reads=[tmpf_r[i], tmpg_r[i]], writes=[tmpf_r[i]])
                                for eo in range(2):
                                    S.op("dve", (lambda e, eo=eo: e.tensor_scalar(out=kst[:, kv, eo, 0:n], in0=tmpf[i][:, 0:n], scalar1=hm_t[:, eo:eo + 1],
                                                                                  scalar2=None, op0=ALU.mult)),
                                         reads=[tmpf_r[i]], writes=[kst_r])
                        return [stage2, stage3, stage4]
                    def v_tail(pend):
                        for blk in range(nblk):
                            bk = 4 + state["mb"] % 4
                            state["mb"] += 1

                            def vm(e, blk=blk, bk=bk):
                                ins = None
                                for kc in range(DC):
                                    ins = e.matmul(ps[bk][:, 0:256], lhsT=hT[:, kc, blk * 128:(blk + 1) * 128], rhs=wv_t[:, kc, :],
                                                   start=(kc == 0), stop=(kc == DC - 1))
                                return ins
                            S.op("pe", vm, reads=hT_r + [wv_r], writes=[psr[bk]])
                            S.op("act", (lambda e, blk=blk, bk=bk: e.activation(out=vst[:, blk, :], in_=ps[bk][:, 0:256], func=ACT.Copy,
                                                                                scale=rstdT[:, blk:blk + 1])),
                                 reads=[psr[bk], rstdT_r], writes=[vst_r])
                            for stages in pend:
                                if stages:
                                    stages.pop(0)()
                            while pend and not pend[0]:
                                pend.pop(0)
                    linear(DC, 512, [wq_b[mb] for mb in range(4)] + [wk_b[0]], (lambda kc: hT[:, kc, 0:n]), hT_r, qk_evac, n, tail=v_tail)
                    S.dma("sp", (lambda e, e0=e0, n=n: e.dma_start(out=qT_s[:, e0:e0 + n].rearrange("(kc p) t -> p kc t", p=128), in_=aT[:, 0:16, 0:n])),
                          reads=aT_r[0:16], writes=[qT_r])
                    S.dma("sp", (lambda e, e0=e0, n=n: e.dma_start(out=kT_s[:, :, :, e0:e0 + n].rearrange("a b p t -> p a b t"), in_=kst[:, :, :, 0:n])),
                          reads=[kst_r], writes=[kT_r])
                    S.dma("sp", (lambda e, e0=e0, n=n, nblk=nblk: e.dma_start(out=v_s[e0:e0 + n, :].rearrange("(b p) m -> p b m", p=128), in_=vst[:, 0:nblk, :])),
                          reads=[vst_r], writes=[v_r])
                    return xq
            for ti, (e0, n) in enumerate(tiles1):
                for _ in range(4):
                    if conv:
                        conv.pop(0)()
                nxt = tiles1[ti + 1][0] if ti + 1 < len(tiles1) else None
                xq = t1_tile(e0, n, xq, ti > 0, nxt)
            while conv:
                conv.pop(0)()
            S.barrier()
            st_t1.close()

            st_t2 = contextlib.ExitStack()

            def sb3(name, shape, dt):
                return st_t2.enter_context(nc.sbuf_tensor(name, list(shape), dt))
            vpad = sb3("vpad", [128, 6, 4, 2, 128], BF16)
            pbf = [sb3("pbf%d" % i, [128, 392], BF16) for i in range(3)]
            dgb = [sb3("dgb%d" % i, [128, 128], BF16) for i in range(3)]
            sinkb_t = sb3("sinkb_t", [128, 32], BF16)
            pTs = [sb3("pTs%d" % i, [128, 384], BF16) for i in range(2)]
            stat = [sb3("stat%d" % i, [128, 4], F32) for i in range(3)]
            mask_t = sb3("mask_t", [128, 3, 384], BF16)
            S.dma("sp", lambda e: e.dma_start(out=mask_t[:], in_=masks), writes=[mask_r])
            S.op("dve", lambda e: e.memset(vpad[:].rearrange("p a b c d -> p (a b c d)"), 0.0), writes=[vpad_r])
            S.op("dve", lambda e: e.tensor_copy(out=sinkb_t[:], in_=sink_t[:]), writes=[mask_r])
            tiles2 = [(128 + t * 512, t * 512, t == 0, t == 7) for t in range(8)] + \
                     [(EXT_P + 128 + t * 512, OWN_P + t * 512, t == 0, t == 1) for t in range(2)]
            n = 512
            aq = 0
            oq = 0
            def load_kv(e0):
                S.dma("sp", (lambda e: e.dma_start(out=kst[:], in_=kT_s[:, :, :, e0 - 128:e0 + 640].rearrange("a b p t -> p a b t"))),
                      reads=[kT_r], writes=[kst_r])
                S.dma("sp", (lambda e: e.dma_start(out=vst[:], in_=v_s[e0 - 128:e0 + 640, :].rearrange("(b p) m -> p b m", p=128))),
                      reads=[v_r], writes=[vst_r])
                for eo in range(2):
                    S.op("pool", (lambda e, eo=eo: e.tensor_copy(out=vpad[:, :, :, eo, eo * 64:(eo + 1) * 64],
                                                                 in_=vst[:].rearrange("p b (k d) -> p b k d", d=64))),
                         reads=[vst_r], writes=[vpad_r])

            def t2_tile(e0, y0, first, last, aq, oq, first_tile, next_e0):
                    if first_tile:
                        load_kv(e0)
                    S.dma("sp", (lambda e, e0=e0: e.dma_start(out=aT[:, 0:16, :], in_=qT_s[:, e0:e0 + 512].rearrange("(kc p) t -> p kc t", p=128))),
                          reads=[qT_r], writes=aT_r[0:16])
                    S.dma("sp", (lambda e, e0=e0: e.dma_start(out=xT[:], in_=x2T_s[:, e0:e0 + 512].rearrange("(kc p) t -> p kc t", p=128))),
                          reads=[x2T_r], writes=xT_r)
                    units = [(blk, hp, eo) for blk in range(4) for hp in range(16) for eo in range(2)]

                    NU = len(units)

                    def st_qk(idx):
                        blk, hp, eo = units[idx]
                        kv, h = hp // 4, 2 * hp + eo
                        bs = idx % 3
                        mi = 0 if (first and blk == 0) else (2 if (last and blk == 3) else 1)

                        def qk(e):
                            e.matmul(ps[bs][:, 384:385], lhsT=idb_t[:], rhs=sinkb_t[:, h:h + 1], start=True, stop=True)
                            e.matmul(ps[bs][:, 0:384], lhsT=idb_t[:], rhs=mask_t[:, mi, :], start=True, stop=False)
                            return e.matmul(ps[bs][:, 0:384], lhsT=aT[:, hp, blk * 128:(blk + 1) * 128],
                                            rhs=kst[:, kv, eo, blk * 128:blk * 128 + 384], start=False, stop=True)
                        S.op("pe", qk, reads=[aT_r[hp], kst_r, mask_r], writes=[psr[bs]])

                    def st_max(idx):
                        i = idx % 3
                        bs = i
                        S.op("dve", (lambda e: e.tensor_reduce(out=stat[i][:, 0:1], in_=ps[bs][:, 0:385], axis=AX.X, op=ALU.max, negate=True)),
                             reads=[psr[bs]], writes=[negm_r[i]])

                    def st_exp(idx):
                        i = idx % 3
                        bs = i
                        S.op("act", (lambda e: e.activation(out=pbf[i][:, 0:385], in_=ps[bs][:, 0:385], func=ACT.Exp, bias=stat[i][:, 0:1],
                                                            scale=1.0, accum_out=stat[i][:, 1:2])),
                             reads=[psr[bs], negm_r[i], rsum_r[i]], writes=[pbf_r[i], rsum_r[i]])

                    def st_diag(idx):
                        i = idx % 3
                        S.op("dve", (lambda e: e.reciprocal(out=stat[i][:, 2:3], in_=stat[i][:, 1:2])),
                             reads=[rsum_r[i]], writes=[rden_r[i]])
                        S.op("dve", (lambda e: e.tensor_scalar(out=dgb[i][:], in0=idb_t[:], scalar1=stat[i][:, 2:3], scalar2=None, op0=ALU.mult)),
                             reads=[rden_r[i]], writes=[dgb_r[i]])

                    def st_tp(idx):
                        i = idx % 3
                        bt = 3 + idx % 2

                        def tp(e):
                            ins = None
                            for j in range(3):
                                ins = e.matmul(ps[bt][:, j * 128:(j + 1) * 128], lhsT=pbf[i][:, j * 128:(j + 1) * 128], rhs=dgb[i][:],
                                               start=True, stop=True)
                            return ins
                        S.op("pe", tp, reads=[pbf_r[i], dgb_r[i]], writes=[psr[bt]])

                    def st_copy(idx):
                        j2 = idx % 2
                        bt = 3 + j2
                        S.op("act", (lambda e: e.activation(out=pTs[j2][:], in_=ps[bt][:, 0:384], func=ACT.Copy)),
                             reads=[psr[bt]], writes=[pTs_r[j2]])

                    def st_pv(idx):
                        blk, hp, eo = units[idx]
                        kv = hp // 4
                        j2 = idx % 2
                        bo = 5 + (idx // 2) % 2

                        def pv(e):
                            ins = None
                            for j in range(3):
                                ins = e.matmul(ps[bo][:, 0:128], lhsT=vpad[:, blk + j, kv, eo, :], rhs=pTs[j2][:, j * 128:(j + 1) * 128],
                                               start=(eo == 0 and j == 0), stop=(eo == 1 and j == 2))
                            return ins
                        S.op("pe", pv, reads=[vpad_r, pTs_r[j2]], writes=[psr[bo]])

                    def st_evac(idx):
                        blk, hp, eo = units[idx]
                        if eo != 1:
                            return
                        bo = 5 + (idx // 2) % 2
                        S.op("dve", (lambda e: e.tensor_copy(out=hT[:, hp, blk * 128:(blk + 1) * 128], in_=ps[bo][:, 0:128])),
                             reads=[psr[bo]], writes=[hT_r[hp]])

                    def ok(i):
                        return 0 <= i < NU
                    for t in range(NU + 5):
                        if ok(t - 2):
                            st_diag(t - 2)
                        if ok(t - 1):
                            st_max(t - 1)
                        if ok(t - 4):
                            st_evac(t - 4)
                        if ok(t - 3):
                            st_copy(t - 3)
                        if ok(t - 1):
                            st_exp(t - 1)
                        if ok(t):
                            st_qk(t)
                        if ok(t - 2):
                            st_tp(t - 2)
                        if ok(t - 3):
                            st_pv(t - 3)
                    if next_e0 is not None:
                        load_kv(next_e0)
                    linear(DC, 512, [wo_b[mb] for mb in range(4)], (lambda kc: hT[:, kc, 0:n]), hT_r, add_evac(n), n)
                    rmsnorm(3, n)
                    mlp(1, n)
                    for blk in range(4):
                        ob = oq % 2
                        oq += 1
                        for d0 in range(0, DC, 4):
                            bk = 4 + state["mb"] % 2
                            state["mb"] += 1

                            def tr(e, blk=blk, d0=d0, bk=bk):
                                ins = None
                                for i in range(4):
                                    ins = e.matmul(ps[bk][:, i * 128:(i + 1) * 128], lhsT=xT[:, d0 + i, blk * 128:(blk + 1) * 128],
                                                   rhs=idf_t[:], start=True, stop=True)
                                return ins
                            S.op("pe", tr, reads=xT_r[d0:d0 + 4], writes=[psr[bk]])
                            if (d0 // 4) % 2 == 0:
                                S.op("act", (lambda e, ob=ob, d0=d0, bk=bk: e.activation(out=xtok[ob][:, d0 * 128:(d0 + 4) * 128], in_=ps[bk][:, 0:512], func=ACT.Copy)),
                                     reads=[psr[bk]], writes=[xtok_r[ob]])
                            else:
                                S.op("dve", (lambda e, ob=ob, d0=d0, bk=bk: e.tensor_copy(out=xtok[ob][:, d0 * 128:(d0 + 4) * 128], in_=ps[bk][:, 0:512])),
                                     reads=[psr[bk]], writes=[xtok_r[ob]])
                        S.dma("sp", (lambda e, ob=ob, r0=y0 + blk * 128: e.dma_start(out=y_own[r0:r0 + 128, :], in_=xtok[ob][:])),
                              reads=[xtok_r[ob]], writes=[y_r])
                    return aq, oq
            for ti, (e0, y0, first, last) in enumerate(tiles2):
                nxt = tiles2[ti + 1][0] if ti + 1 < len(tiles2) else None
                aq, oq = t2_tile(e0, y0, first, last, aq, oq, ti == 0, nxt)
            S.barrier()
            st_t2.close()

            with nc.Block() as block:
                S.emit(block)
    return nc


def _bf(a):
    return np.ascontiguousarray(a.astype(ml_dtypes.bfloat16))


def _host_tables(j):
    t = {}
    for nm, N2 in (("wdft_p", N2_P), ("wdft_s", N2_S)):
        a = 2.0 * np.pi * np.outer(np.arange(N2), np.arange(N2)) / N2
        t[nm] = _bf(np.concatenate([np.cos(a), -np.sin(a)], axis=1))
    for nm, N2, K1, N, k1s in (("g_p", N2_P, K1_P, SEQ_P, 32 * j - 1), ("g_s", N2_S, K1_S, SEQ_S, 32 * j - 4)):
        k1 = (k1s + np.arange(K1)) % 128
        k = (N2 * k1[None, :] + np.arange(N2)[:, None]).astype(np.int64)
        n1 = np.arange(128, dtype=np.int64)
        ph = (n1[:, None, None] * k[None, :, :]) % N
        th = 2.0 * np.pi * ph.astype(np.float64) / N
        gr, gi = np.cos(th), -np.sin(th)
        t[nm] = _bf(np.concatenate([-gi, gr, gi], axis=2))
    c = (np.arange(2)[None, :] * 128 + np.arange(128)[:, None]).astype(np.int64)
    cp = np.arange(256, dtype=np.int64)
    th = 2.0 * np.pi * ((c[:, :, None] * cp[None, None, :]) % 256).astype(np.float64) / 256.0
    t["ctab"] = np.ascontiguousarray(np.stack([np.cos(th), np.sin(th)], axis=2).astype(np.float32))
    t["ident_f"] = np.eye(128, dtype=np.float32)
    t["ident_b"] = _bf(np.eye(128))
    t["ones_d"] = _bf(np.full((128, 128), 1.0 / D))
    oh = np.zeros((128, 128))
    oh[:64, :64] = 1.0 / 64
    oh[64:, 64:] = 1.0 / 64
    t["ones_h"] = _bf(oh)
    rt = np.zeros((128, 128))
    for hb in (0, 64):
        for d in range(8):
            rt[hb + d + 8, hb + d] = -1.0
            rt[hb + d, hb + d + 8] = 1.0
    t["rot_t"] = _bf(rt)
    hm = np.zeros((128, 2), np.float32)
    hm[:64, 0] = 1.0
    hm[64:, 1] = 1.0
    t["halfmask"] = hm
    t["attn_scale"] = np.full((128, 1), 0.125, np.float32)
    qi = np.arange(128)[:, None]
    kj = np.arange(384)[None, :]
    band = np.abs(qi + 128 - kj) <= 128
    m = np.zeros((128, 3, 384), np.float32)
    for mi in range(3):
        ok = band.copy()
        if mi == 0 and j == 0:
            ok &= kj >= 128
        if mi == 2 and j == 3:
            ok &= kj < 256
        m[:, mi, :] = np.where(ok, 0.0, NEGM)
    t["masks"] = _bf(m)
    pos = np.concatenate([(OWN_P * j - 128 + np.arange(EXT_P)) % SEQ_P, (OWN_S * j - 128 + np.arange(EXT_S)) % SEQ_S]).astype(np.float32)
    inv = (np.float32(500000.0) ** (-np.arange(0, 16, 2, dtype=np.float32) / np.float32(16))).astype(np.float32)
    ang = pos[None, :] * inv[:, None]
    cs = np.zeros((128, 2, EXT), np.float32)
    cs[:, 0, :] = 1.0
    for hb in (0, 64):
        for d in range(16):
            cs[hb + d, 0, :] = np.cos(ang[d % 8])
            cs[hb + d, 1, :] = np.sin(ang[d % 8])
    t["ropecs"] = cs
    return t


_NC_CACHE = {}


def kernel(x_prompt, x_sample, fourier_norm, fourier_w_out, attn_norm, attn_w_qkv, attn_q_norm,
           attn_k_norm, attn_sink, attn_w_o, mlp_norm, mlp_w_up, mlp_w_down):
    f32 = np.float32
    x_prompt = np.asarray(x_prompt, f32)
    x_sample = np.asarray(x_sample, f32)

    def fm(g):
        return np.asarray(g, f32).reshape(DC, 128).T
    gains = np.ascontiguousarray(np.stack([fm(fourier_norm[0]), fm(mlp_norm[0]), fm(attn_norm[0]), fm(mlp_norm[1])], axis=1))
    qk_gain = np.ascontiguousarray(np.stack([np.tile(np.asarray(attn_q_norm[0], f32), 2), np.tile(np.asarray(attn_k_norm[0], f32), 2)], axis=1))
    sinkb = np.ascontiguousarray(np.broadcast_to(np.asarray(attn_sink[0], f32)[None, :], (128, 32)))
    shared = dict(
        w_fout=np.ascontiguousarray(np.asarray(fourier_w_out[0], f32)),
        w_qkv=np.ascontiguousarray(np.asarray(attn_w_qkv[0], f32)),
        w_o=np.ascontiguousarray(np.asarray(attn_w_o[0], f32)),
        w_up=np.ascontiguousarray(np.asarray(mlp_w_up, f32)),
        w_down=np.ascontiguousarray(np.asarray(mlp_w_down, f32)),
        gains=gains, qk_gain=qk_gain, sinkb=sinkb,
    )
    in_maps = []
    for c in range(8):
        b, j = c // 4, c % 4
        ip = (OWN_P * j - 128 + np.arange(EXT_P)) % SEQ_P
        isx = (OWN_S * j - 128 + np.arange(EXT_S)) % SEQ_S
        m = dict(shared)
        m["xp_full"] = x_prompt[b]
        m["xs_full"] = x_sample[b]
        m["x_own"] = np.ascontiguousarray(np.concatenate([x_prompt[b][ip], x_sample[b][isx]], axis=0))
        m.update(_host_tables(j))
        in_maps.append(m)
    if "nc" not in _NC_CACHE:
        _NC_CACHE["nc"] = build_program()
    res = run_bass_kernel_spmd(_NC_CACHE["nc"], in_maps, core_ids=list(range(8)))
    y_p = np.empty((2, SEQ_P, D), f32)
    y_s = np.empty((2, SEQ_S, D), f32)
    for c in range(8):
        b, j = c // 4, c % 4
        y = np.asarray(res.results[c]["y_own"])
        y_p[b, OWN_P * j:OWN_P * (j + 1)] = y[:OWN_P]
        y_s[b, OWN_S * j:OWN_S * (j + 1)] = y[OWN_P:]
    return (y_p, y_s)
```
